# Optimizing a Trainium2 kernel written in Bass

```python
import math, functools
import jax, jax.numpy as jnp
from jax import lax
import numpy as np

D_MODEL = 1024
BATCH = 16
SEQ = 2048
DEPTH = 2
DEC_BATCH = 8
DEC_SEQ = 16
PAST_LEN = 2048

CHUNK = 64
Q_BLOCK = 128
N_MIXERS = 2
H_A = 8
DH_A = 64
DV_A = 2 * DH_A
D_A = H_A * DV_A
H_B = 16
DH_B = 64
D_B = H_B * DH_B
RMS_EPS = 1e-6

kernel_name = 'streaming_diff_stickbreak_hybrid_step'


def rms_norm(x, g):
    x32 = x.astype(jnp.float32)
    y = x32 * lax.rsqrt(jnp.mean(x32 * x32, axis=-1, keepdims=True) + RMS_EPS)
    return (y * g.astype(jnp.float32)).astype(x.dtype)


def alibi_slopes(n_heads):
    return 2.0 ** (-8.0 * jnp.arange(1, n_heads + 1, dtype=jnp.float32) / n_heads)


def diff_lambda_init(layer):
    return 0.8 - 0.6 * math.exp(-0.3 * layer)


def diff_core(q, k, v, q_pos, k_pos, lam, slopes):
    s = jnp.einsum('bqhcd,bkhcd->bchqk', q.astype(jnp.float32), k.astype(jnp.float32)) * (DH_A ** -0.5)
    dist = jnp.abs(q_pos[:, None] - k_pos[None, :]).astype(jnp.float32)
    visible = (k_pos[None, :] // CHUNK) <= (q_pos[:, None] // CHUNK)
    s = jnp.where(visible, s - slopes[:, None, None] * dist, -jnp.inf)
    p = jax.nn.softmax(s, axis=-1)
    a = p[:, 0] - lam * p[:, 1]
    return jnp.einsum('bhqk,bkhe->bqhe', a, v.astype(jnp.float32))


def stick_breaking_core(q, k, v, q_pos, k_pos):
    z = jnp.einsum('bqhd,bkhd->bhqk', q.astype(jnp.float32), k.astype(jnp.float32)) * (DH_B ** -0.5)
    before = k_pos[None, :] < q_pos[:, None]
    log_beta = jax.nn.log_sigmoid(z)
    log_keep = jnp.where(before, log_beta - z, 0.0)
    log_keep_after = lax.cumsum(log_keep, axis=3, reverse=True) - log_keep
    a = jnp.where(before, jnp.exp(log_beta + log_keep_after), 0.0)
    return jnp.einsum('bhqk,bkhd->bqhd', a, v.astype(jnp.float32))


def sweep_query_blocks(core, q, k, v):
    seq = q.shape[1]
    outs = []
    for i in range(seq // Q_BLOCK):
        lo, hi = i * Q_BLOCK, (i + 1) * Q_BLOCK
        outs.append(core(q[:, lo:hi], k[:, :hi], v[:, :hi], jnp.arange(lo, hi), jnp.arange(hi)))
    return jnp.concatenate(outs, axis=1)


def diff_layer(x, norm_g, w_in, q_norm, k_norm, lambda_q1, lambda_k1, lambda_q2, lambda_k2,
               subln_g, w_out, layer, past_k=None, past_v=None):
    b, s, _ = x.shape
    h = rms_norm(x, norm_g)
    q, k, v, gate = jnp.split(h @ w_in, [D_A, 2 * D_A, 3 * D_A], axis=-1)
    q = rms_norm(q.reshape(b, s, H_A, 2, DH_A), q_norm)
    k = rms_norm(k.reshape(b, s, H_A, 2, DH_A), k_norm)
    v = v.reshape(b, s, H_A, DV_A)
    lam_init = diff_lambda_init(layer)
    f32 = jnp.float32
    lam = (jnp.exp(jnp.sum(lambda_q1.astype(f32) * lambda_k1.astype(f32)))
           - jnp.exp(jnp.sum(lambda_q2.astype(f32) * lambda_k2.astype(f32))) + lam_init)
    core = functools.partial(diff_core, lam=lam, slopes=alibi_slopes(H_A))
    if past_k is None:
        o = sweep_query_blocks(core, q, k, v)
    else:
        p_len = past_k.shape[1]
        k_all = jnp.concatenate([past_k.reshape(b, p_len, H_A, 2, DH_A), k], axis=1)
        v_all = jnp.concatenate([past_v, v], axis=1)
        o = core(q, k_all, v_all, p_len + jnp.arange(s), jnp.arange(p_len + s))
    o = rms_norm(o, subln_g) * (1.0 - lam_init)
    y = (o.reshape(b, s, D_A).astype(x.dtype) * jax.nn.silu(gate)) @ w_out
    return x + y, k.reshape(b, s, H_A, 2 * DH_A), v


def stick_breaking_layer(x, norm_g, w_in, w_out, past_k=None, past_v=None):
    b, s, _ = x.shape
    h = rms_norm(x, norm_g)
    q, k, v, gate = jnp.split(h @ w_in, [D_B, 2 * D_B, 3 * D_B], axis=-1)
    q = q.reshape(b, s, H_B, DH_B)
    k = k.reshape(b, s, H_B, DH_B)
    v = v.reshape(b, s, H_B, DH_B)
    if past_k is None:
        o = sweep_query_blocks(stick_breaking_core, q, k, v)
    else:
        p_len = past_k.shape[1]
        k_all = jnp.concatenate([past_k, k], axis=1)
        v_all = jnp.concatenate([past_v, v], axis=1)
        o = stick_breaking_core(q, k_all, v_all, p_len + jnp.arange(s), jnp.arange(p_len + s))
    y = (o.reshape(b, s, D_B).astype(x.dtype) * jax.nn.silu(gate)) @ w_out
    return x + y, k, v


def setup_inputs(seed: int = 0) -> dict:
    key = jax.random.key(seed)
    ks = jax.random.split(key, 20)
    f32 = jnp.float32
    nrm = lambda k, shape, scale: scale * jax.random.normal(k, shape, f32)
    return {
        'x_prompt': nrm(ks[0], (BATCH, SEQ, D_MODEL), 1.0),
        'x_sample': nrm(ks[1], (DEC_BATCH, DEC_SEQ, D_MODEL), 1.0),
        'cache_k_0': nrm(ks[2], (DEC_BATCH, PAST_LEN, H_A, 2 * DH_A), 1.0),
        'cache_v_0': nrm(ks[3], (DEC_BATCH, PAST_LEN, H_A, DV_A), 1.0),
        'cache_k_1': nrm(ks[4], (DEC_BATCH, PAST_LEN, H_B, DH_B), 1.0),
        'cache_v_1': nrm(ks[5], (DEC_BATCH, PAST_LEN, H_B, DH_B), 1.0),
        'norm_g_0': 1.0 + nrm(ks[6], (D_MODEL,), 0.02),
        'w_in_0': nrm(ks[7], (D_MODEL, 4 * D_A), D_MODEL ** -0.5),
        'q_norm_0': 1.0 + nrm(ks[8], (DH_A,), 0.02),
        'k_norm_0': 1.0 + nrm(ks[9], (DH_A,), 0.02),
        'lambda_q1_0': nrm(ks[10], (DH_A,), 0.1),
        'lambda_k1_0': nrm(ks[11], (DH_A,), 0.1),
        'lambda_q2_0': nrm(ks[12], (DH_A,), 0.1),
        'lambda_k2_0': nrm(ks[13], (DH_A,), 0.1),
        'subln_g_0': 1.0 + nrm(ks[14], (DV_A,), 0.02),
        'w_out_0': nrm(ks[15], (D_A, D_MODEL), D_A ** -0.5),
        'norm_g_1': 1.0 + nrm(ks[16], (D_MODEL,), 0.02),
        'w_in_1': nrm(ks[17], (D_MODEL, 4 * D_B), D_MODEL ** -0.5),
        'w_out_1': nrm(ks[18], (D_B, D_MODEL), D_B ** -0.5),
    }


def reference(x_prompt, x_sample, cache_k_0, cache_v_0, cache_k_1, cache_v_1,
              norm_g_0, w_in_0, q_norm_0, k_norm_0, lambda_q1_0, lambda_k1_0,
              lambda_q2_0, lambda_k2_0, subln_g_0, w_out_0,
              norm_g_1, w_in_1, w_out_1):
    diff_params = (norm_g_0, w_in_0, q_norm_0, k_norm_0, lambda_q1_0, lambda_k1_0,
                   lambda_q2_0, lambda_k2_0, subln_g_0, w_out_0)
    sb_params = (norm_g_1, w_in_1, w_out_1)
    layer_inputs = ((diff_params, cache_k_0, cache_v_0), (sb_params, cache_k_1, cache_v_1))
    y_prompt, y_sample = x_prompt, x_sample
    new_state = []
    for layer in range(DEPTH):
        params, ck, cv = layer_inputs[layer]
        if layer % N_MIXERS == 0:
            y_prompt, kp, vp = diff_layer(y_prompt, *params, layer=layer)
            y_sample, ks, vs = diff_layer(y_sample, *params, layer=layer, past_k=ck, past_v=cv)
        else:
            y_prompt, kp, vp = stick_breaking_layer(y_prompt, *params)
            y_sample, ks, vs = stick_breaking_layer(y_sample, *params, past_k=ck, past_v=cv)
        new_state.append((kp, vp, ks, vs))
    (k0_prompt, v0_prompt, k0_sample, v0_sample), (k1_prompt, v1_prompt, k1_sample, v1_sample) = new_state
    return (y_prompt, y_sample, k0_prompt, v0_prompt, k0_sample, v0_sample,
            k1_prompt, v1_prompt, k1_sample, v1_sample)
```

```python
import numpy as np
import ml_dtypes
from contextlib import ExitStack
import concourse.bass as bass
import concourse.mybir as mybir
from concourse.bass_utils import run_bass_kernel_spmd

F32 = mybir.dt.float32
BF16 = mybir.dt.bfloat16
AF = mybir.ActivationFunctionType
ALU = mybir.AluOpType
AX = mybir.AxisListType

NCORES = 8
D = 1024
S = 2048
PAST = 2048
TS = 16
NG = 8
EPS = 1e-6
LAM_INIT0 = 0.2
NEGBIG = -30000.0
OPT_PJ2 = False
OPT_SUMO = False

C_ID = 0
C_BO = 128
C_ON = 256
C_NT = 384
C_NEG1 = 512
C_ONE = 640
C_BD = 768
C_BDS = C_BD + 8 * 128
C_M1 = C_BDS + 8 * 16
C_M2 = C_M1 + 512
C_TRI = C_M2 + 512
NC16 = C_TRI + 32
F_ID = 0
F_BT = 128
F_BS = 256
NF32 = 384


def _host_constants():
    c16 = np.zeros((128, NC16), np.float32)
    c16[:, C_ID:C_ID + 128] = np.eye(128)
    bo = np.zeros((128, 128), np.float32)
    bo[:64, :64] = 1.0 / 64
    bo[64:, 64:] = 1.0 / 64
    c16[:, C_BO:C_BO + 128] = bo
    c16[:, C_ON:C_ON + 128] = 1.0 / 128
    j = np.arange(128)[:, None]
    i = np.arange(128)[None, :]
    c16[:, C_NT:C_NT + 128] = np.where(j >= i, -1.0, 0.0)
    c16[:, C_NEG1:C_NEG1 + 128] = -1.0
    c16[:, C_ONE:C_ONE + 128] = 1.0
    slopes = np.array([2.0 ** -(h + 1) for h in range(8)], np.float64)
    vis = ~((j >= 64) & (i < 64))
    for h in range(8):
        bd = slopes[h] * (-np.abs(i - j) + (i - 64))
        c16[:, C_BD + h * 128:C_BD + (h + 1) * 128] = np.where(vis, bd, NEGBIG)
    j16 = np.arange(16)[:, None]
    i16 = np.arange(16)[None, :]
    for h in range(8):
        c16[:16, C_BDS + h * 16:C_BDS + (h + 1) * 16] = slopes[h] * (-np.abs(i16 - j16) + (i16 - 8))
    tri = (j < i).astype(np.float32)
    m1 = np.concatenate([np.zeros((128, 128), np.float32), tri], axis=1)
    m2 = np.concatenate([tri, np.ones((128, 128), np.float32)], axis=1)
    c16[:, C_M1:C_M1 + 512] = (np.concatenate([m1, m1], axis=1) - 1.0) * (-NEGBIG)
    c16[:, C_M2:C_M2 + 512] = (np.concatenate([m2, m2], axis=1) - 1.0) * (-NEGBIG)
    tri16 = (j16 < i16).astype(np.float32)
    c16[:16, C_TRI:C_TRI + 32] = (np.concatenate([tri16, tri16], axis=1) - 1.0) * (-NEGBIG)
    cf = np.zeros((128, NF32), np.float32)
    cf[:, F_ID:F_ID + 128] = np.eye(128)
    p = np.arange(128)[:, None]
    for h in range(8):
        for d in range(16):
            cf[:, F_BT + h * 16 + d] = (slopes[h] * (p[:, 0] - 64 - 128 * d))
            cf[:, F_BS + h * 16 + d] = (slopes[h] * (d * 128 + p[:, 0] - (PAST + 8)))
    return c16.astype(ml_dtypes.bfloat16), cf.astype(np.float32)


class Slot:
    __slots__ = ("name", "last_w", "readers", "dma_sem", "dma_cnt")

    def __init__(self, name):
        self.name = name
        self.last_w = None
        self.readers = {}
        self.dma_sem = None
        self.dma_cnt = 0


class Prog:
    ENGS = ("pe", "act", "dve", "pool", "sp")

    def __init__(self, nc, es):
        self.nc = nc
        self.es = es
        self.sem = {e: es.enter_context(nc.semaphore("s_" + e)) for e in ("pe", "act", "dve", "pool")}
        self.cnt = {e: 0 for e in self.ENGS}
        self.lists = {e: [] for e in self.ENGS}
        self.waited = {e: {} for e in self.ENGS}
        self.dma_slots = {}
        self.nslot = 0

    def slot(self, name):
        self.nslot += 1
        return Slot("%s_%d" % (name, self.nslot))

    def _semof(self, key):
        if isinstance(key, str):
            return self.sem[key]
        return self.dma_slots[key[1]].dma_sem

    def _deps(self, eng, reads, writes):
        deps = {}

        def add(ev, raw):
            if ev is None:
                return
            key, val = ev
            if key == eng and eng == "pe":
                return
            if deps.get(key, 0) < val:
                deps[key] = val

        for s in reads:
            add(s.last_w, True)
        for s in writes:
            add(s.last_w, False)
            for ev in s.readers.values():
                add(ev, False)
        waits = []
        w = self.waited[eng]
        for key, val in deps.items():
            if w.get(key, 0) < val:
                w[key] = val
                waits.append((key, val))
        return waits

    def op(self, eng, fn, reads=(), writes=()):
        waits = self._deps(eng, reads, writes)
        self.cnt[eng] += 1
        ev = (eng, self.cnt[eng])
        self.lists[eng].append((waits, fn))
        for s in writes:
            s.last_w = ev
            s.readers = {}
        for s in reads:
            if s.last_w is not ev:
                s.readers[eng] = ev

    def dma(self, q, out, in_, reads=(), writes=()):
        waits = self._deps(q, reads, writes)
        slot = (list(writes) + list(reads))[0]
        if slot.dma_sem is None:
            slot.dma_sem = self.es.enter_context(self.nc.semaphore("d_" + slot.name))
            self.dma_slots[slot.name] = slot
        slot.dma_cnt += 1
        ev = (("dma", slot.name), 16 * slot.dma_cnt)
        self.lists[q].append((waits, ("dma", out, in_, slot)))
        for s in writes:
            s.last_w = ev
            s.readers = {}
        for s in reads:
            s.readers["dma"] = ev

    def finalize(self):
        waits = []
        for slot in self.dma_slots.values():
            key = ("dma", slot.name)
            val = 16 * slot.dma_cnt
            if self.waited["sp"].get(key, 0) < val:
                waits.append((key, val))
        self.lists["sp"].append((waits, None))

    def emit(self, block):
        engmap = {"pe": block.tensor, "act": block.scalar, "dve": block.vector,
                  "pool": block.gpsimd, "sp": block.sync}
        for e in self.ENGS:
            items = self.lists[e]

            def body(eng, items=items, e=e):
                for waits, fn in items:
                    for key, val in waits:
                        eng.wait_ge(self._semof(key), val)
                    if fn is None:
                        continue
                    if isinstance(fn, tuple):
                        _, out, in_, slot = fn
                        eng.dma_start(out=out, in_=in_).then_inc(slot.dma_sem, 16)
                    else:
                        fn(eng).then_inc(self.sem[e], 1)

            engmap[e](body)


def build_program():
    nc = bass.Bass("TRN2", target_bir_lowering=False)

    def din(name, shape, dt=F32):
        return nc.dram_tensor(name, list(shape), dt, kind="ExternalInput").ap()

    def dout(name, shape):
        return nc.dram_tensor(name, list(shape), F32, kind="ExternalOutput").ap()

    xp = din("xp", [2, S, D])
    xs = din("xs", [TS, D])
    ck = [din("ck0", [PAST, D]), din("ck1", [PAST, D])]
    cv = [din("cv0", [PAST, D]), din("cv1", [PAST, D])]
    win = [din("win0", [NG, D, 512]), din("win1", [NG, D, 512])]
    wout = [din("wout0", [D, D]), din("wout1", [D, D])]
    gtd = [din("gt0", [128, D]), din("gt1", [128, D])]
    colsd = din("cols", [128, 4])
    lamd = din("lamv", [128, 4, 64])
    c16d = din("c16", [128, NC16], BF16)
    cfd = din("cf32", [128, NF32])

    yp = dout("yp", [2, S, D])
    ys = dout("ys", [TS, D])
    kop = [dout("k0p", [2, S, D]), dout("k1p", [2, S, D])]
    vop = [dout("v0p", [2, S, D]), dout("v1p", [2, S, D])]
    kos = [dout("k0s", [TS, D]), dout("k1s", [TS, D])]
    vos = [dout("v0s", [TS, D]), dout("v1s", [TS, D])]

    with ExitStack() as es:
        P = Prog(nc, es)

        def sb(name, shape, dt):
            return es.enter_context(nc.sbuf_tensor("sb_" + name, list(shape), dt))

        x_sb = sb("x_sb", [128, 16, D], F32)
        X = [P.slot("x%d" % t) for t in range(16)]
        hT = sb("hT", [128, 8, S], BF16)
        HT = P.slot("hT")
        qpad = [[sb("qpad%d_%d" % (s_, c), [128, S], BF16) for c in range(2)] for s_ in range(2)]
        QP = [[P.slot("qpad%d_%d" % (s_, b)) for b in range(4)] for s_ in range(2)]
        kT = [sb("kT%d" % s_, [128, PAST + TS], BF16) for s_ in range(2)]
        KT = [[P.slot("kT%d_%d" % (s_, b)) for b in range(5)] for s_ in range(2)]
        Vb = [sb("Vb%d" % s_, [128, 17, 128], BF16) for s_ in range(2)]
        VS = [[P.slot("V%d_%d" % (s_, b)) for b in range(5)] for s_ in range(2)]
        gT = [sb("gT%d" % s_, [128, S], BF16) for s_ in range(2)]
        GT = [[P.slot("gT%d_%d" % (s_, b)) for b in range(4)] for s_ in range(2)]
        wi = [sb("wi0", [128, 8, 512], BF16), sb("wi1", [128, 8, 512], BF16)]
        WI = [P.slot("wi0"), P.slot("wi1")]
        wo = [sb("wo%d" % i, [128, D], BF16) for i in range(3)]
        WO = [P.slot("wo%d" % i) for i in range(3)]
        c16 = sb("c16", [128, NC16], BF16)
        cf = sb("cf", [128, NF32], F32)
        cols = sb("cols", [128, 4], F32)
        lamv = sb("lamv", [128, 4, 64], F32)
        CONST = P.slot("const")
        sc = sb("sc", [128, 16], F32)
        SC = P.slot("sc")
        ss = sb("ss", [128, 4], F32)
        SS = P.slot("ss")
        NW32, NW16 = 10, 10
        w32all = sb("w32all", [128, NW32, 512], F32)
        w32 = [w32all[:, i, :] for i in range(NW32)]
        W32 = [P.slot("w32_%d" % i) for i in range(NW32)]
        w16 = [sb("w16_%d" % i, [128, 512], BF16) for i in range(NW16)]
        W16 = [P.slot("w16_%d" % i) for i in range(NW16)]
        xs32 = w32all[:, 6:8, :].rearrange("p a b -> p (a b)")
        XS = [W32[6], W32[7]]
        gtile = w32all[:, 8:10, :].rearrange("p a b -> p (a b)")
        GTL = [W32[8], W32[9]]
        NST = 3
        stg = [sb("stg%d" % i, [128, 512], F32) for i in range(NST)]
        STG = [P.slot("stg%d" % i) for i in range(NST)]
        ps = [es.enter_context(nc.psum_tensor("ps%d" % i, [128, 512], F32)) for i in range(8)]
        PS = [P.slot("ps%d" % i) for i in range(8)]
        PJ, AUX = 6, 7

        ident16 = c16[:, C_ID:C_ID + 128]
        identf = cf[:, F_ID:F_ID + 128]
        NEGLAM, GQ8, GK, GSUB = 0, 1, 2, 3
        IK = 6
        ISQ = (8, 9)

        cnt = {"stg": 0}
        from collections import deque
        fg = deque()
        bg = deque()

        P.dma("sp", c16[:], c16d[:, :], writes=[CONST])
        P.dma("sp", cf[:], cfd[:, :], writes=[CONST])
        P.dma("sp", cols[:], colsd[:, :], writes=[CONST])
        P.dma("sp", lamv[:], lamd[:, :, :], writes=[CONST])
        for s_ in range(2):
            for c in range(2):
                P.op("pool", lambda e, s_=s_, c=c: e.memset(qpad[s_][c][:], 0.0), writes=QP[s_])
        P.op("dve", lambda e: e.tensor_tensor(out=w32[0][:, 0:64], in0=lamv[:, 0, :], in1=lamv[:, 1, :], op=ALU.mult),
             reads=[CONST], writes=[W32[0]])
        P.op("dve", lambda e: e.tensor_tensor(out=w32[0][:, 64:128], in0=lamv[:, 2, :], in1=lamv[:, 3, :], op=ALU.mult),
             reads=[CONST], writes=[W32[0]])
        P.op("dve", lambda e: e.tensor_reduce(out=sc[:, 4:6], in_=w32[0][:, 0:128].rearrange("p (a b) -> p a b", b=64),
                                              axis=AX.X, op=ALU.add), reads=[W32[0]], writes=[SC])
        P.op("act", lambda e: e.activation(out=sc[:, 6:8], in_=sc[:, 4:6], func=AF.Exp), reads=[SC], writes=[SC])
        P.op("dve", lambda e: e.tensor_tensor(out=sc[:, 8:9], in0=sc[:, 7:8], in1=sc[:, 6:7], op=ALU.subtract),
             reads=[SC], writes=[SC])
        P.op("dve", lambda e: e.tensor_scalar(out=sc[:, NEGLAM:NEGLAM + 1], in0=sc[:, 8:9], scalar1=-LAM_INIT0,
                                              scalar2=None, op0=ALU.add), reads=[SC], writes=[SC])
        P.op("dve", lambda e: e.tensor_scalar(out=sc[:, GQ8:GQ8 + 1], in0=cols[:, 0:1], scalar1=0.125, scalar2=None,
                                              op0=ALU.mult), reads=[CONST, SC], writes=[SC])
        P.op("dve", lambda e: e.tensor_copy(out=sc[:, GK:GK + 1], in_=cols[:, 1:2]), reads=[CONST, SC], writes=[SC])
        P.op("dve", lambda e: e.tensor_scalar(out=sc[:, GSUB:GSUB + 1], in0=cols[:, 2:3], scalar1=1.0 - LAM_INIT0,
                                              scalar2=None, op0=ALU.mult), reads=[CONST, SC], writes=[SC])

        class Seq:
            pass

        seqs = []
        for b in range(2):
            q = Seq()
            q.sample = False
            q.T = S
            q.ntile = 16
            q.xsrc = xp[b]
            q.yout = yp[b]
            q.kout = [kop[0][b], kop[1][b]]
            q.vout = [vop[0][b], vop[1][b]]
            q.koff = 0
            q.kt0 = 0
            seqs.append(q)
        q = Seq()
        q.sample = True
        q.T = TS
        q.ntile = 1
        q.xsrc = xs
        q.yout = ys
        q.kout = kos
        q.vout = vos
        q.koff = PAST
        q.kt0 = 16
        seqs.append(q)

        def tp(seq, t):
            return 16 if seq.sample else 128

        work = [(si, layer, g) for si in range(len(seqs)) for layer in range(2) for g in range(NG)]

        def load_w(idx):
            if idx >= len(work):
                return
            si, layer, g = work[idx]
            P.dma("pool", wi[idx % 2][:], win[layer][g].rearrange("(kc p) n -> p kc n", p=128), writes=[WI[idx % 2]])
            P.dma("pool", wo[idx % 3][:], wout[layer][g * 128:(g + 1) * 128, :], writes=[WO[idx % 3]])

        def recip(out, in_, reads, writes, bias=0.0):
            P.op("act", lambda e: e.activation(out=out, in_=in_, func=AF.Ln, bias=bias), reads=reads, writes=writes)
            P.op("act", lambda e: e.activation(out=out, in_=out, func=AF.Exp, scale=-1.0), reads=writes, writes=writes)

        def phase_a(seq, layer):
            P.dma("sp", gtile[:, :], gtd[layer][:, :], writes=GTL)
            for t in range(seq.ntile):
                p_ = tp(seq, t)
                if layer == 0:
                    P.dma("sp", x_sb[:p_, t, :], seq.xsrc[t * 128:t * 128 + p_, :], writes=[X[t]])
                P.op("act", lambda e, t=t, p_=p_: e.activation(out=xs32[:p_, :], in_=x_sb[:p_, t, :], func=AF.Square,
                                                              accum_out=ss[:p_, 0:1]),
                     reads=[X[t]], writes=XS + [SS])
                P.op("act", lambda e, p_=p_: e.activation(out=ss[:p_, 1:2], in_=ss[:p_, 0:1], func=AF.Ln,
                                                          scale=1.0 / D, bias=EPS), reads=[SS], writes=[SS])
                P.op("act", lambda e, p_=p_: e.activation(out=ss[:p_, 2:3], in_=ss[:p_, 1:2], func=AF.Exp, scale=-0.5),
                     reads=[SS], writes=[SS])
                P.op("dve", lambda e, t=t, p_=p_: e.scalar_tensor_tensor(
                    out=xs32[:p_, :], in0=x_sb[:p_, t, :], scalar=ss[:p_, 2:3], in1=gtile[:p_, :],
                    op0=ALU.mult, op1=ALU.mult), reads=[X[t], SS] + GTL, writes=XS)
                for hb in range(2):
                    bank = 6 + hb
                    for k4 in range(4):
                        kc = hb * 4 + k4
                        P.op("pe", lambda e, bank=bank, k4=k4, kc=kc, p_=p_: e.transpose(
                            ps[bank][:, k4 * 128:k4 * 128 + p_], xs32[:p_, kc * 128:(kc + 1) * 128],
                            identf[:p_, :p_]), reads=XS + [CONST], writes=[PS[bank]])
                    src = ps[bank][:, :].rearrange("p (a b) -> p a b", b=128)[:, :, :p_]
                    dst = hT[:, hb * 4:hb * 4 + 4, t * 128:t * 128 + p_]
                    if hb == 0:
                        P.op("act", lambda e, src=src, dst=dst: e.activation(out=dst, in_=src, func=AF.Copy),
                             reads=[PS[bank]], writes=[HT])
                    else:
                        P.op("dve", lambda e, src=src, dst=dst: e.tensor_copy(out=dst, in_=src),
                             reads=[PS[bank]], writes=[HT])

        def rstd_from_ms(bank_ms, nt, l_i, r_i):
            P.op("act", lambda e: e.activation(out=w32[l_i][:, :nt], in_=ps[bank_ms][:, :nt], func=AF.Ln, bias=EPS),
                 reads=[PS[bank_ms]], writes=[W32[l_i]])
            P.op("act", lambda e: e.activation(out=w32[r_i][:, :nt], in_=w32[l_i][:, :nt], func=AF.Exp, scale=-0.5),
                 reads=[W32[l_i]], writes=[W32[r_i]])

        def in_proj_tasks(seq, layer, g, widx):
            st_ = widx % 2
            wb = wi[widx % 2]
            WB = WI[widx % 2]
            T = seq.T
            tasks = []
            qp, kTs, Vs, gTs = qpad[st_], kT[st_], Vb[st_], gT[st_]
            AUX = 7
            TRB = 6

            def next_stg():
                i = cnt["stg"] % NST
                cnt["stg"] += 1
                return i

            if seq.sample:
                ckv = ck[layer].rearrange("(t p) f -> p t f", p=128)
                cvv = cv[layer].rearrange("(t p) f -> p t f", p=128)
                tasks.append(lambda: P.dma("pool", Vs[:, 0:16, :], cvv[:, :, g * 128:(g + 1) * 128], writes=VS[st_][0:4]))

                def cache_chunk(c4):
                    sidx = next_stg()
                    P.dma("sp", stg[sidx][:, :].rearrange("p (a b) -> p a b", b=128),
                          ckv[:, c4 * 4:c4 * 4 + 4, g * 128:(g + 1) * 128], writes=[STG[sidx]])
                    tb_ = 6 + c4 % 2
                    for i4 in range(4):
                        P.op("pe", lambda e, i4=i4: e.transpose(
                            ps[tb_][:, i4 * 128:(i4 + 1) * 128], stg[sidx][:, i4 * 128:(i4 + 1) * 128], identf[:, :]),
                            reads=[STG[sidx], CONST], writes=[PS[tb_]])
                    P.op("dve", lambda e: e.tensor_copy(out=kTs[:, c4 * 512:(c4 + 1) * 512], in_=ps[tb_][:, :]),
                         reads=[PS[tb_]], writes=[KT[st_][c4]])
                for c4 in range(4):
                    tasks.append(lambda c4=c4: cache_chunk(c4))

            def mm_proj(t0, nt, col0, PJ, k0, k1):
                for kc in range(k0, k1):
                    P.op("pe", lambda e, kc=kc: e.matmul(
                        ps[PJ][:, :nt], wb[:, kc, col0:col0 + 128], hT[:, kc, t0:t0 + nt],
                        start=(kc == 0), stop=(kc == 7)), reads=[WB, HT], writes=[PS[PJ]])

            def gate_a(t0, nt, PJ, IREC):
                P.op("act", lambda e: e.activation(out=w32[IREC][:, :nt], in_=ps[PJ][:, :nt], func=AF.Exp, scale=-1.0),
                     reads=[PS[PJ]], writes=[W32[IREC]])
                P.op("act", lambda e: e.activation(out=w32[IREC][:, :nt], in_=w32[IREC][:, :nt], func=AF.Ln, bias=1.0),
                     reads=[W32[IREC]], writes=[W32[IREC]])
                P.op("act", lambda e: e.activation(out=w32[IREC][:, :nt], in_=w32[IREC][:, :nt], func=AF.Exp,
                                                   scale=-1.0), reads=[W32[IREC]], writes=[W32[IREC]])

            def gate_c(t0, nt, PJ, IREC):
                P.op("dve", lambda e: e.tensor_tensor(out=gTs[:, t0:t0 + nt], in0=ps[PJ][:, :nt], in1=w32[IREC][:, :nt],
                                                      op=ALU.mult), reads=[PS[PJ], W32[IREC]], writes=[GT[st_][t0 // 512]])

            def norm_sq(nt, sqi, PJ):
                P.op("act", lambda e: e.activation(out=w16[sqi][:, :nt], in_=ps[PJ][:, :nt], func=AF.Square),
                     reads=[PS[PJ]], writes=[W16[sqi]])

            def norm_ms(nt, sqi, MS):
                P.op("pe", lambda e: e.matmul(ps[MS][:, :nt], c16[:, C_BO:C_BO + 128], w16[sqi][:, :nt],
                                              start=True, stop=True), reads=[W16[sqi], CONST], writes=[PS[MS]])

            def norm_b(nt, MS, il):
                P.op("act", lambda e: e.activation(out=w32[il][:, :nt], in_=ps[MS][:, :nt], func=AF.Ln, bias=EPS),
                     reads=[PS[MS]], writes=[W32[il]])

            def norm_c(nt, il):
                P.op("act", lambda e: e.activation(out=w32[il][:, :nt], in_=w32[il][:, :nt], func=AF.Exp, scale=-0.5),
                     reads=[W32[il]], writes=[W32[il]])

            def q0_c(t0, nt, PJ, ir):
                for c in range(2):
                    r0, r1 = c * 64, (c + 1) * 64
                    P.op("dve", lambda e, c=c, r0=r0, r1=r1: e.scalar_tensor_tensor(
                        out=qp[c][r0:r1, t0:t0 + nt], in0=ps[PJ][r0:r1, :nt], scalar=sc[r0:r1, GQ8:GQ8 + 1],
                        in1=w32[ir][r0:r1, :nt], op0=ALU.mult, op1=ALU.mult),
                        reads=[PS[PJ], SC, W32[ir]], writes=[QP[st_][t0 // 512]])

            def k0_c1(t0, nt, PJ, ir):
                P.op("dve", lambda e: e.scalar_tensor_tensor(
                    out=w32[IK][:, :nt], in0=ps[PJ][:, :nt], scalar=sc[:, GK:GK + 1], in1=w32[ir][:, :nt],
                    op0=ALU.mult, op1=ALU.mult), reads=[PS[PJ], SC, W32[ir]], writes=[W32[IK]])

            def k0_c(t0, nt, tk):
                P.op("act", lambda e: e.activation(out=kTs[:, tk:tk + nt], in_=w32[IK][:, :nt], func=AF.Copy),
                     reads=[W32[IK]], writes=[KT[st_][tk // 512]])
                ntile = (nt + 127) // 128
                p_ = min(nt, 128)
                for i4 in range(ntile):
                    P.op("pe", lambda e, i4=i4: e.transpose(
                        ps[TRB][:p_, i4 * 128:(i4 + 1) * 128], w32[IK][:, i4 * 128:i4 * 128 + p_], identf[:, :]),
                        reads=[W32[IK], CONST], writes=[PS[TRB]])

            def k0_d(t0, nt):
                ntile = (nt + 127) // 128
                p_ = min(nt, 128)
                si_ = next_stg()
                P.op("dve", lambda e: e.tensor_copy(out=stg[si_][:p_, :ntile * 128], in_=ps[TRB][:p_, :ntile * 128]),
                     reads=[PS[TRB]], writes=[STG[si_]])
                if seq.sample:
                    dst = seq.kout[layer][0:p_, g * 128:(g + 1) * 128]
                    srcv = stg[si_][:p_, 0:128]
                else:
                    dst = seq.kout[layer].rearrange("(n p) f -> p n f", p=128)[
                        :, t0 // 128:t0 // 128 + ntile, g * 128:(g + 1) * 128]
                    srcv = stg[si_][:, :ntile * 128].rearrange("p (n f) -> p n f", f=128)
                P.dma("sp", dst, srcv, reads=[STG[si_]])

            def q1_b(t0, nt, PJ):
                for c in range(2):
                    r0, r1 = c * 64, (c + 1) * 64
                    P.op("dve", lambda e, c=c, r0=r0, r1=r1: e.tensor_scalar(
                        out=qp[c][r0:r1, t0:t0 + nt], in0=ps[PJ][r0:r1, :nt], scalar1=0.125, scalar2=None,
                        op0=ALU.mult), reads=[PS[PJ]], writes=[QP[st_][t0 // 512]])

            def k1_b(t0, nt, tk, PJ):
                P.op("dve", lambda e: e.tensor_copy(out=kTs[:, tk:tk + nt], in_=ps[PJ][:, :nt]),
                     reads=[PS[PJ]], writes=[KT[st_][tk // 512]])

            def v0_mm(t0, i4, p_):
                for kc in range(8):
                    P.op("pe", lambda e, kc=kc: e.matmul(
                        ps[AUX][:p_, i4 * 128:(i4 + 1) * 128], hT[:, kc, t0 + i4 * 128:t0 + i4 * 128 + p_],
                        wb[:, kc, 256:384], start=(kc == 0), stop=(kc == 7)),
                        reads=[WB, HT], writes=[PS[AUX]])

            def v0_evac(t0, nt):
                ntile = (nt + 127) // 128
                p_ = min(nt, 128)
                si_ = next_stg()
                P.op("dve", lambda e: e.tensor_copy(out=stg[si_][:p_, :ntile * 128], in_=ps[AUX][:p_, :ntile * 128]),
                     reads=[PS[AUX]], writes=[STG[si_]])
                kt_ = seq.kt0 + t0 // 128
                P.op("pool", lambda e: e.tensor_copy(
                    out=Vs[:p_, kt_:kt_ + ntile, :],
                    in_=stg[si_][:p_, :ntile * 128].rearrange("p (n f) -> p n f", f=128)),
                    reads=[STG[si_]], writes=[VS[st_][kt_ // 4]])
                if seq.sample:
                    dst = seq.vout[layer][0:p_, g * 128:(g + 1) * 128]
                    srcv = stg[si_][:p_, 0:128]
                else:
                    dst = seq.vout[layer].rearrange("(n p) f -> p n f", p=128)[
                        :, t0 // 128:t0 // 128 + ntile, g * 128:(g + 1) * 128]
                    srcv = stg[si_][:, :ntile * 128].rearrange("p (n f) -> p n f", f=128)
                P.dma("sp", dst, srcv, reads=[STG[si_]])

            def kv1_mm(t0, i4, j, p_, KB):
                for kc in range(8):
                    P.op("pe", lambda e, kc=kc: e.matmul(
                        ps[KB][:p_, j * 256:(j + 1) * 256], hT[:, kc, t0 + i4 * 128:t0 + i4 * 128 + p_],
                        wb[:, kc, 128:384], start=(kc == 0), stop=(kc == 7)),
                        reads=[WB, HT], writes=[PS[KB]])

            def kv1_evac(t0, i2, n2, p_, KB):
                si_ = next_stg()
                P.op("dve", lambda e: e.tensor_copy(out=stg[si_][:p_, :n2 * 256], in_=ps[KB][:p_, :n2 * 256]),
                     reads=[PS[KB]], writes=[STG[si_]])
                kt_ = seq.kt0 + t0 // 128 + i2
                sview = stg[si_][:p_, :n2 * 256].rearrange("p (n f) -> p n f", f=256)
                P.op("pool", lambda e: e.tensor_copy(out=Vs[:p_, kt_:kt_ + n2, :], in_=sview[:, :, 128:256]),
                     reads=[STG[si_]], writes=[VS[st_][kt_ // 4]])
                if seq.sample:
                    P.dma("sp", seq.kout[layer][0:p_, g * 128:(g + 1) * 128], stg[si_][:p_, 0:128], reads=[STG[si_]])
                    P.dma("sp", seq.vout[layer][0:p_, g * 128:(g + 1) * 128], stg[si_][:p_, 128:256],
                          reads=[STG[si_]])
                else:
                    n0 = t0 // 128 + i2
                    kd = seq.kout[layer].rearrange("(n p) f -> p n f", p=128)[:, n0:n0 + n2, g * 128:(g + 1) * 128]
                    vd = seq.vout[layer].rearrange("(n p) f -> p n f", p=128)[:, n0:n0 + n2, g * 128:(g + 1) * 128]
                    P.dma("sp", kd, sview[:, :, 0:128], reads=[STG[si_]])
                    P.dma("sp", vd, sview[:, :, 128:256], reads=[STG[si_]])

            def A(lst, f, *a):
                lst.append(lambda: f(*a))

            MSB = 6
            blocks = []
            for bi_, t0 in enumerate(range(0, T, 512)):
                nt = min(512, T - t0)
                blocks.append(dict(t0=t0, nt=nt, tk=seq.koff + t0, ntile=(nt + 127) // 128, p_=min(nt, 128),
                                   par=bi_ % 2))

            def banks(bk):
                par = bk["par"]
                return (3 * par, 3 * par + 1, 3 * par + 2), (3 * par, 3 * par + 1, 3 * par + 2)

            def emit_block(bk, prev):
                t0, nt, tk, ntile, p_ = bk["t0"], bk["nt"], bk["tk"], bk["ntile"], bk["p_"]
                (pq, pk_, pg), (il, ir, irec) = banks(bk)
                A(tasks, mm_proj, t0, nt, 0, pq, 0, 8)
                A(tasks, mm_proj, t0, nt, 128, pk_, 0, 8)
                if prev is not None:
                    tail_dve(prev)
                if layer == 0:
                    A(tasks, norm_sq, nt, ISQ[0], pq)
                    A(tasks, norm_ms, nt, ISQ[0], MSB)
                    A(tasks, norm_b, nt, MSB, il)
                A(tasks, mm_proj, t0, nt, 384, pg, 0, 8)
                if layer == 0:
                    for i4 in range(ntile):
                        A(tasks, v0_mm, t0, i4, p_)
                    if prev is not None:
                        A(tasks, k0_c, prev["t0"], prev["nt"], prev["tk"])
                        A(tasks, k0_d, prev["t0"], prev["nt"])
                    A(tasks, norm_sq, nt, ISQ[1], pk_)
                    A(tasks, gate_a, t0, nt, pg, irec)
                    A(tasks, norm_ms, nt, ISQ[1], MSB)
                    A(tasks, norm_b, nt, MSB, ir)
                    A(tasks, norm_c, nt, il)
                    A(tasks, norm_c, nt, ir)
                    A(tasks, v0_evac, t0, nt)
                else:
                    kbs = []
                    for i2 in range(0, ntile, 2):
                        n2 = min(2, ntile - i2)
                        kb = 6 + (i2 // 2) % 2
                        for j in range(n2):
                            A(tasks, kv1_mm, t0, i2 + j, j, p_, kb)
                        kbs.append((i2, n2, kb))
                    A(tasks, gate_a, t0, nt, pg, irec)
                    for (i2, n2, kb) in kbs:
                        A(tasks, kv1_evac, t0, i2, n2, p_, kb)

            def tail_dve(bk):
                t0, nt, tk = bk["t0"], bk["nt"], bk["tk"]
                (pq, pk_, pg), (il, ir, irec) = banks(bk)
                if layer == 0:
                    A(tasks, q0_c, t0, nt, pq, il)
                    A(tasks, k0_c1, t0, nt, pk_, ir)
                else:
                    A(tasks, q1_b, t0, nt, pq)
                    A(tasks, k1_b, t0, nt, tk, pk_)
                A(tasks, gate_c, t0, nt, pg, irec)

            prev = None
            for bk in blocks:
                emit_block(bk, prev)
                prev = bk
            tail_dve(prev)
            if layer == 0:
                A(tasks, k0_c, prev["t0"], prev["nt"], prev["tk"])
                A(tasks, k0_d, prev["t0"], prev["nt"])
            return tasks

        def out_proj_member(widx, otile, tq, c0, nq, yb):
            for half in range(2):
                P.op("pe", lambda e, half=half: e.matmul(
                    ps[yb][:nq, :], w16[otile][:, c0:c0 + nq], wo[widx % 3][:, half * 512:(half + 1) * 512],
                    start=True, stop=True), reads=[W16[otile], WO[widx % 3]], writes=[PS[yb]])
                P.op("dve", lambda e, half=half: e.tensor_tensor(
                    out=x_sb[:nq, tq, half * 512:(half + 1) * 512], in0=ps[yb][:nq, :],
                    in1=x_sb[:nq, tq, half * 512:(half + 1) * 512], op=ALU.add),
                    reads=[PS[yb], X[tq]], writes=[X[tq]])

        def out_proj_mm(widx, otile, c0, nq, yb, half):
            P.op("pe", lambda e: e.matmul(
                ps[yb][:nq, :], w16[otile][:, c0:c0 + nq], wo[widx % 3][:, half * 512:(half + 1) * 512],
                start=True, stop=True), reads=[W16[otile], WO[widx % 3]], writes=[PS[yb]])

        def out_proj_add(tq, nq, yb, half):
            P.op("dve", lambda e: e.tensor_tensor(
                out=x_sb[:nq, tq, half * 512:(half + 1) * 512], in0=ps[yb][:nq, :],
                in1=x_sb[:nq, tq, half * 512:(half + 1) * 512], op=ALU.add),
                reads=[PS[yb], X[tq]], writes=[X[tq]])

        def queue_out_proj(widx, ot, members, yb, dly, tag):
            for (tq, c0, nq) in members:
                for half in range(2):
                    fg_add(dly, lambda c0=c0, nq=nq, half=half: out_proj_mm(widx, ot, c0, nq, yb, half), tag)
                    fg_add(dly + 1, lambda tq=tq, nq=nq, half=half: out_proj_add(tq, nq, yb, half), tag)
                    dly += 2

        gtick = [0]

        def fg_add(delay, fn, tag=None):
            due = gtick[0] + delay
            if fg and fg[-1][0] > due:
                due = fg[-1][0]
            fg.append((due, fn, tag))

        def flush_fg(tag=None):
            if tag is not None and not any(t == tag for (_, _, t) in fg):
                return
            while fg:
                _, fn, t = fg.popleft()
                fn()
                if tag is not None and not any(t2 == tag for (_, _, t2) in fg):
                    break

        pq = [deque(), deque()]

        def pq_add(par, delay, fn):
            due = gtick[0] + delay
            if pq[par] and pq[par][-1][0] > due:
                due = pq[par][-1][0]
            pq[par].append((due, fn))

        def pq_flush(par):
            while pq[par]:
                pq[par].popleft()[1]()

        hoist = []

        def run_pipeline(steps, stages):
            n = len(steps)
            k = len(stages)
            nticks = n + k - 1
            for tick in range(nticks):
                for si in reversed(range(k)):
                    i = tick - si
                    if 0 <= i < n:
                        stages[si](steps[i], i)
                gtick[0] += 1
                while fg and fg[0][0] <= gtick[0]:
                    fg.popleft()[1]()
                for par in range(2):
                    while pq[par] and pq[par][0][0] <= gtick[0]:
                        pq[par].popleft()[1]()
            while hoist:
                hoist.pop(0)()
            pq_flush(0)
            pq_flush(1)
            flush_fg()

        def attn0(seq, g, widx):
            st_ = widx % 2
            qp, kTs, Vs, gTs = qpad[st_], kT[st_], Vb[st_], gT[st_]
            steps = []
            if not seq.sample:
                for a in range(0, 16, 2):
                    blk = dict(q0=a * 128, members=[(a, 0, 128), (a + 1, 128, 128)], nq=128, nm=2,
                               ob=[(2, 3), (6, 7)][(a // 2) % 2], par=(a // 2) % 2)
                    for d in range(0, a + 2):
                        if d == 0:
                            pairs = [(a, 0), (a + 1, 1)]
                            kind = "diag"
                        elif d <= a:
                            pairs = [(a - d, 0), (a + 1 - d, 1)]
                            kind = "off"
                        else:
                            pairs = [(0, 1)]
                            kind = "off"
                        first = {0: d == 0, 1: d == 0}
                        last = {0: d == a, 1: d == a + 1}
                        steps.append(dict(blk=blk, pairs=pairs, kind=kind, pk=128,
                                          bias=cf[:, F_BT + g * 16 + d:F_BT + g * 16 + d + 1],
                                          first=first, last=last, sfirst=(d == 0), slast=(d == a + 1)))
            else:
                blk = dict(q0=0, members=[(0, 0, 16)], nq=16, nm=1, ob=(2, 3), par=0)
                for kt in range(16):
                    steps.append(dict(blk=blk, pairs=[(kt, 0)], kind="off", pk=128,
                                      bias=cf[:, F_BS + g * 16 + kt:F_BS + g * 16 + kt + 1],
                                      first={0: kt == 0}, last={0: False}, sfirst=(kt == 0), slast=False))
                steps.append(dict(blk=blk, pairs=[(16, 0)], kind="diag", pk=16, bias=None,
                                  first={0: False}, last={0: True}, sfirst=False, slast=True))

            def region(st):
                nq_ = st["blk"]["nq"]
                ms = [m for (_, m) in st["pairs"]]
                return min(ms) * 2 * nq_, (max(ms) + 1) * 2 * nq_

            def s1(st, i):
                sbk = i % 2
                nq_ = st["blk"]["nq"]
                pk = st["pk"]
                q0 = st["blk"]["q0"]
                diag = st["kind"] == "diag"
                if st["sfirst"]:
                    pq_flush(st["blk"]["par"])
                for (kt, m) in st["pairs"]:
                    for c in range(2):
                        col = (m * 2 + c) * nq_
                        qc = slice(q0 + m * nq_, q0 + (m + 1) * nq_)
                        P.op("pe", lambda e, col=col, kt=kt, c=c, qc=qc: e.matmul(
                            ps[sbk][:pk, col:col + nq_], kTs[:, kt * 128:kt * 128 + pk], qp[c][:, qc],
                            start=True, stop=(not diag)), reads=[KT[st_][kt // 4], QP[st_][q0 // 512]], writes=[PS[sbk]])
                        if diag:
                            if pk == 128:
                                bd = c16[:, C_BD + g * 128:C_BD + (g + 1) * 128]
                            else:
                                bd = c16[:16, C_BDS + g * 16:C_BDS + (g + 1) * 16]
                            P.op("pe", lambda e, col=col, bd=bd: e.matmul(
                                ps[sbk][:pk, col:col + nq_], ident16[:pk, :pk], bd, start=False, stop=True),
                                reads=[CONST], writes=[PS[sbk]])

            def s2(st, i):
                sbk = i % 2
                eb = i % 4
                pk = st["pk"]
                r0, r1 = region(st)
                if st["kind"] == "diag":
                    P.op("act", lambda e: e.activation(out=w16[eb][:pk, r0:r1], in_=ps[sbk][:pk, r0:r1], func=AF.Exp),
                         reads=[PS[sbk]], writes=[W16[eb]])
                else:
                    P.op("act", lambda e: e.activation(out=w16[eb][:pk, r0:r1], in_=ps[sbk][:pk, r0:r1], func=AF.Exp,
                                                       bias=st["bias"][:pk, :]),
                         reads=[PS[sbk], CONST], writes=[W16[eb]])

            def s3(st, i):
                eb = i % 4
                pk = st["pk"]
                nq_ = st["blk"]["nq"]
                if not OPT_SUMO:
                    r0, r1 = region(st)
                    for (kt, m) in st["pairs"]:
                        ob = st["blk"]["ob"][m]
                        P.op("pe", lambda e, kt=kt, m=m, ob=ob: e.matmul(
                            ps[ob][:, 0:2 * nq_], Vs[:pk, kt, :], w16[eb][:pk, m * 2 * nq_:(m + 1) * 2 * nq_],
                            start=st["first"][m], stop=st["last"][m]), reads=[VS[st_][kt // 4], W16[eb]], writes=[PS[ob]])
                    sb_ = 4 + st["blk"]["par"]
                    P.op("pe", lambda e: e.matmul(ps[sb_][:, r0:r1], c16[:pk, C_ONE:C_ONE + 128], w16[eb][:pk, r0:r1],
                                                  start=st["sfirst"], stop=st["slast"]),
                         reads=[CONST, W16[eb]], writes=[PS[sb_]])
                    if st["slast"]:
                        epilogue(st["blk"])
                    return
                for (kt, m) in st["pairs"]:
                    ob = 2 + m
                    P.op("pe", lambda e, kt=kt, m=m, ob=ob: e.matmul(
                        ps[ob][:, 0:2 * nq_], Vs[:pk, kt, :], w16[eb][:pk, m * 2 * nq_:(m + 1) * 2 * nq_],
                        start=st["first"][m], stop=False), reads=[VS[st_][kt // 4], W16[eb]], writes=[PS[ob]])
                    P.op("pe", lambda e, m=m, ob=ob: e.matmul(
                        ps[ob][:, 256:256 + 2 * nq_], c16[:pk, C_ONE:C_ONE + 128],
                        w16[eb][:pk, m * 2 * nq_:(m + 1) * 2 * nq_], start=False, stop=st["last"][m]),
                        reads=[CONST, W16[eb]], writes=[PS[ob]])
                if st["slast"]:
                    epilogue(st["blk"])

            def epilogue(blk):
                nq_ = blk["nq"]
                nm = blk["nm"]
                nqt = nq_ * nm
                q0 = blk["q0"]
                par = blk["par"]
                e0, e1, e2, e3, e4 = [par * 5 + j for j in range(5)]
                sqt = 6 + par
                ot = 4 + par
                ob = blk["ob"]
                sbk = 4 + par
                b_ms = ob[0]
                ybank = (ob[1], ob[0])
                pq_flush(par)
                P.op("dve", lambda e: e.tensor_copy(out=w32[e0][:, :2 * nqt], in_=ps[sbk][:, :2 * nqt]),
                     reads=[PS[sbk]], writes=[W32[e0]])
                for m in range(nm):
                    P.op("dve", lambda e, m=m: e.tensor_copy(
                        out=w32[e1][:, m * 2 * nq_:(m + 1) * 2 * nq_], in_=ps[ob[m]][:, 0:2 * nq_]),
                        reads=[PS[ob[m]]], writes=[W32[e1]])

                def p_rs():
                    P.op("act", lambda e: e.activation(out=w32[e0][:, :2 * nqt], in_=w32[e0][:, :2 * nqt], func=AF.Ln),
                         reads=[W32[e0]], writes=[W32[e0]])
                    P.op("act", lambda e: e.activation(out=w32[e0][:, :2 * nqt], in_=w32[e0][:, :2 * nqt], func=AF.Exp,
                                                       scale=-1.0), reads=[W32[e0]], writes=[W32[e0]])

                def p_comb():
                    P.op("dve", lambda e: e.tensor_tensor(out=w32[e1][:, :2 * nqt], in0=w32[e1][:, :2 * nqt],
                                                          in1=w32[e0][:, :2 * nqt], op=ALU.mult),
                         reads=[W32[e1], W32[e0]], writes=[W32[e1]])
                    t4 = w32[e1][:, :2 * nqt].rearrange("p (m c q) -> p m c q", m=nm, c=2)
                    ocv = w32[e2][:, :nqt].rearrange("p (m q) -> p m q", m=nm)
                    P.op("dve", lambda e: e.scalar_tensor_tensor(out=ocv, in0=t4[:, :, 1, :],
                                                                 scalar=sc[:, NEGLAM:NEGLAM + 1],
                                                                 in1=t4[:, :, 0, :], op0=ALU.mult, op1=ALU.add),
                         reads=[W32[e1], SC], writes=[W32[e2]])
                    P.op("dve", lambda e: e.tensor_tensor(out=w16[sqt][:, :nqt], in0=w32[e2][:, :nqt],
                                                          in1=w32[e2][:, :nqt], op=ALU.mult),
                         reads=[W32[e2]], writes=[W16[sqt]])

                def p_ms():
                    P.op("pe", lambda e: e.matmul(ps[b_ms][:, :nqt], c16[:, C_ON:C_ON + 128], w16[sqt][:, :nqt],
                                                  start=True, stop=True), reads=[CONST, W16[sqt]], writes=[PS[b_ms]])

                def p_r():
                    P.op("act", lambda e: e.activation(out=w32[e3][:, :nqt], in_=ps[b_ms][:, :nqt], func=AF.Ln,
                                                       bias=EPS), reads=[PS[b_ms]], writes=[W32[e3]])
                    P.op("act", lambda e: e.activation(out=w32[e3][:, :nqt], in_=w32[e3][:, :nqt], func=AF.Exp,
                                                       scale=-0.5), reads=[W32[e3]], writes=[W32[e3]])

                def p_og():
                    P.op("dve", lambda e: e.scalar_tensor_tensor(out=w32[e4][:, :nqt], in0=w32[e2][:, :nqt],
                                                                 scalar=sc[:, GSUB:GSUB + 1], in1=w32[e3][:, :nqt],
                                                                 op0=ALU.mult, op1=ALU.mult),
                         reads=[W32[e2], SC, W32[e3]], writes=[W32[e4]])
                    P.op("pool", lambda e: e.tensor_tensor(out=w16[ot][:, :nqt], in0=w32[e4][:, :nqt],
                                                           in1=gTs[:, q0:q0 + nqt], op=ALU.mult),
                         reads=[W32[e4], GT[st_][q0 // 512]], writes=[W16[ot]])

                pq_add(par, 2, p_rs)
                pq_add(par, 3, p_comb)
                pq_add(par, 5, p_ms)
                pq_add(par, 6, p_r)
                pq_add(par, 8, p_og)
                dly = 10
                for (tq, c0, nq) in blk["members"]:
                    for half in range(2):
                        yb = ybank[half]
                        pq_add(par, dly + half, lambda c0=c0, nq=nq, half=half, yb=yb: out_proj_mm(
                            widx, ot, c0, nq, yb, half))
                    for half in range(2):
                        yb = ybank[half]
                        pq_add(par, dly + 2 + half, lambda tq=tq, nq=nq, half=half, yb=yb: out_proj_add(
                            tq, nq, yb, half))
                    dly += 4

            run_pipeline(steps, [s1, s2, s3])

        def attn1(seq, g, widx):
            st_ = widx % 2
            qp, kTs, Vs, gTs = qpad[st_], kT[st_], Vb[st_], gT[st_]
            steps = []
            if not seq.sample:
                for a in range(0, 16, 2):
                    blk = dict(q0=a * 128, members=[(a, 0, 128), (a + 1, 128, 128)], nqt=256, bi=a // 2)
                    for j, kt in enumerate(range(a + 1, -1, -1)):
                        mask = None
                        if kt == a + 1:
                            mask = c16[:, C_M1:C_M1 + 512]
                        elif kt == a:
                            mask = c16[:, C_M2:C_M2 + 512]
                        steps.append(dict(blk=blk, kt=kt, pk=128, mask=mask, first=(kt == a + 1), last=(kt == 0), j=j))
            else:
                blk = dict(q0=0, members=[(0, 0, 16)], nqt=16, bi=0)
                steps.append(dict(blk=blk, kt=16, pk=16, mask=c16[:16, C_TRI:C_TRI + 32], first=True, last=False, j=0))
                for j, kt in enumerate(range(15, -1, -1)):
                    steps.append(dict(blk=blk, kt=kt, pk=128, mask=None, first=False, last=(kt == 0), j=j + 1))
            def s1(st, i):
                ab = i % 3
                nqt = st["blk"]["nqt"]
                q0 = st["blk"]["q0"]
                pk = st["pk"]
                kt = st["kt"]
                for h in range(2):
                    P.op("pe", lambda e, h=h: e.matmul(
                        ps[ab][:pk, h * nqt:(h + 1) * nqt], kTs[:, kt * 128:kt * 128 + pk], qp[h][:, q0:q0 + nqt],
                        start=(h == 0), stop=False, skip_group_check=True),
                        reads=[KT[st_][kt // 4], QP[st_][q0 // 512]], writes=[PS[ab]])
                if st["mask"] is not None:
                    P.op("pe", lambda e: e.matmul(ps[ab][:pk, :2 * nqt], ident16[:pk, :pk], st["mask"],
                                                  start=False, stop=False, skip_group_check=True),
                         reads=[CONST], writes=[PS[ab]])

            def s2(st, i):
                ab = i % 3
                eb = (0, 1, 6)[i % 3]
                spb = (0, 1, 2, 8)[i % 4]
                pk = st["pk"]
                n2 = 2 * st["blk"]["nqt"]
                P.op("act", lambda e: e.activation(out=w32[eb][:pk, :n2], in_=ps[ab][:pk, :n2], func=AF.Exp),
                     reads=[PS[ab]], writes=[W32[eb]])
                P.op("act", lambda e: e.activation(out=w16[spb][:pk, :n2], in_=w32[eb][:pk, :n2], func=AF.Ln, bias=1.0),
                     reads=[W32[eb]], writes=[W16[spb]])

            def s3(st, i):
                ab = i % 3
                spb = (0, 1, 2, 8)[i % 4]
                CB = (3, 6)[i % 2]
                pk = st["pk"]
                n2 = 2 * st["blk"]["nqt"]
                P.op("pe", lambda e: e.matmul(ps[ab][:pk, :n2], c16[:pk, C_NT:C_NT + pk], w16[spb][:pk, :n2],
                                              start=False, stop=True, skip_group_check=True),
                     reads=[CONST, W16[spb]], writes=[PS[ab]])
                if not st["last"]:
                    P.op("pe", lambda e: e.matmul(ps[CB][:, :n2], c16[:pk, C_NEG1:C_NEG1 + 128], w16[spb][:pk, :n2],
                                                  start=True, stop=True), reads=[CONST, W16[spb]], writes=[PS[CB]])

            def s4(st, i):
                ab = i % 3
                tb = (2, 3, 7)[i % 3]
                ra, rb = 4 + st["j"] % 2, 4 + (st["j"] + 1) % 2
                CB = (3, 6)[i % 2]
                pk = st["pk"]
                n2 = 2 * st["blk"]["nqt"]
                if st["first"]:
                    P.op("dve", lambda e: e.tensor_copy(out=w32[tb][:pk, :n2], in_=ps[ab][:pk, :n2]),
                         reads=[PS[ab]], writes=[W32[tb]])
                else:
                    P.op("dve", lambda e: e.tensor_tensor(out=w32[tb][:pk, :n2], in0=ps[ab][:pk, :n2],
                                                          in1=w32[ra][:pk, :n2], op=ALU.add),
                         reads=[PS[ab], W32[ra]], writes=[W32[tb]])
                if not st["last"]:
                    if st["first"]:
                        P.op("dve", lambda e: e.tensor_copy(out=w32[rb][:, :n2], in_=ps[CB][:, :n2]),
                             reads=[PS[CB]], writes=[W32[rb]])
                    else:
                        P.op("dve", lambda e: e.tensor_tensor(out=w32[rb][:, :n2], in0=ps[CB][:, :n2],
                                                              in1=w32[ra][:, :n2], op=ALU.add),
                             reads=[PS[CB], W32[ra]], writes=[W32[rb]])

            def s5(st, i):
                tb = (2, 3, 7)[i % 3]
                atb = (3, 4, 5, 9)[i % 4]
                pk = st["pk"]
                n2 = 2 * st["blk"]["nqt"]
                P.op("act", lambda e: e.activation(out=w16[atb][:pk, :n2], in_=w32[tb][:pk, :n2], func=AF.Exp),
                     reads=[W32[tb]], writes=[W16[atb]])

            def s6(st, i):
                atb = (3, 4, 5, 9)[i % 4]
                pk = st["pk"]
                nqt = st["blk"]["nqt"]
                n2 = 2 * nqt
                kt = st["kt"]
                ob = 4 + st["blk"]["bi"] % 2
                P.op("pe", lambda e: e.matmul(ps[ob][:, :n2], Vs[:pk, kt, :], w16[atb][:pk, :n2],
                                              start=st["first"], stop=st["last"]),
                     reads=[VS[st_][kt // 4], W16[atb]], writes=[PS[ob]])
                if st["last"]:
                    q0 = st["blk"]["q0"]
                    ot = 6 + st["blk"]["bi"] % 2
                    tag = ("ep1", st["blk"]["bi"] % 2)
                    flush_fg(tag)

                    def p_gate():
                        for h in range(2):
                            r0, r1 = h * 64, (h + 1) * 64
                            P.op("dve", lambda e, h=h, r0=r0, r1=r1: e.tensor_tensor(
                                out=w16[ot][r0:r1, :nqt], in0=ps[ob][r0:r1, h * nqt:(h + 1) * nqt],
                                in1=gTs[r0:r1, q0:q0 + nqt], op=ALU.mult),
                                reads=[PS[ob], GT[st_][q0 // 512]], writes=[W16[ot]])

                    pq_add(st["blk"]["bi"] % 2, 1, p_gate)
                    queue_out_proj(widx, ot, st["blk"]["members"], 7, 2, tag)

            run_pipeline(steps, [s1, s2, s3, s4, s5, s6])

        load_w(0)
        load_w(1)
        widx = 0
        for si, seq in enumerate(seqs):
            for layer in range(2):
                phase_a(seq, layer)
                nxt = None
                for g in range(NG):
                    tl = nxt if nxt is not None else in_proj_tasks(seq, layer, g, widx)
                    for t in tl:
                        t()
                    load_w(widx + 2)
                    nxt = None
                    if g + 1 < NG and not seq.sample:
                        nxt = in_proj_tasks(seq, layer, g + 1, widx + 1)
                        hoist.extend(nxt[:2])
                        nxt = nxt[2:]
                    if layer == 0:
                        attn0(seq, g, widx)
                    else:
                        attn1(seq, g, widx)
                    widx += 1
            for t in range(seq.ntile):
                p_ = tp(seq, t)
                P.dma("sp", seq.yout[t * 128:t * 128 + p_, :], x_sb[:p_, t, :], reads=[X[t]])

        P.finalize()
        block = es.enter_context(nc.Block())
        P.emit(block)
    return nc


def _regroup_win(w):
    w = np.asarray(w, np.float32)
    parts = [w[:, j * D:(j + 1) * D].reshape(D, NG, 128) for j in range(4)]
    out = np.stack(parts, axis=2)
    return np.ascontiguousarray(out.transpose(1, 0, 2, 3).reshape(NG, D, 512))


def kernel(x_prompt, x_sample, cache_k_0, cache_v_0, cache_k_1, cache_v_1,
           norm_g_0, w_in_0, q_norm_0, k_norm_0, lambda_q1_0, lambda_k1_0,
           lambda_q2_0, lambda_k2_0, subln_g_0, w_out_0, norm_g_1, w_in_1, w_out_1):
    f32 = np.float32
    x_prompt = np.asarray(x_prompt, f32)
    x_sample = np.asarray(x_sample, f32)
    c16, cf = _host_constants()
    win0 = _regroup_win(w_in_0)
    win1 = _regroup_win(w_in_1)
    wout0 = np.ascontiguousarray(np.asarray(w_out_0, f32))
    wout1 = np.ascontiguousarray(np.asarray(w_out_1, f32))
    gt0 = np.ascontiguousarray(np.broadcast_to(np.asarray(norm_g_0, f32)[None, :], (128, D)))
    gt1 = np.ascontiguousarray(np.broadcast_to(np.asarray(norm_g_1, f32)[None, :], (128, D)))
    cols = np.zeros((128, 4), f32)
    cols[:, 0] = np.tile(np.asarray(q_norm_0, f32), 2)
    cols[:, 1] = np.tile(np.asarray(k_norm_0, f32), 2)
    cols[:, 2] = np.asarray(subln_g_0, f32)
    lam = np.stack([np.asarray(v, f32) for v in (lambda_q1_0, lambda_k1_0, lambda_q2_0, lambda_k2_0)], axis=0)
    lamv = np.ascontiguousarray(np.broadcast_to(lam[None], (128, 4, 64)))
    caches = [np.asarray(c, f32).reshape(NCORES, PAST, D) for c in (cache_k_0, cache_v_0, cache_k_1, cache_v_1)]
    in_maps = []
    for c in range(NCORES):
        in_maps.append({
            "xp": np.ascontiguousarray(x_prompt[2 * c:2 * c + 2]),
            "xs": np.ascontiguousarray(x_sample[c]),
            "ck0": np.ascontiguousarray(caches[0][c]), "cv0": np.ascontiguousarray(caches[1][c]),
            "ck1": np.ascontiguousarray(caches[2][c]), "cv1": np.ascontiguousarray(caches[3][c]),
            "win0": win0, "win1": win1, "wout0": wout0, "wout1": wout1,
            "gt0": gt0, "gt1": gt1, "cols": cols, "lamv": lamv, "c16": c16, "cf32": cf,
        })
    nc = build_program()
    res = run_bass_kernel_spmd(nc, in_maps, core_ids=list(range(NCORES)))
    r = res.results

    def cat(name, shape):
        return np.concatenate([np.asarray(r[c][name], f32) for c in range(NCORES)], axis=0).reshape(shape)

    def stack(name, shape):
        return np.stack([np.asarray(r[c][name], f32) for c in range(NCORES)], axis=0).reshape(shape)

    y_prompt = cat("yp", (16, S, D))
    y_sample = stack("ys", (8, TS, D))
    k0p = cat("k0p", (16, S, 8, 128))
    v0p = cat("v0p", (16, S, 8, 128))
    k0s = stack("k0s", (8, TS, 8, 128))
    v0s = stack("v0s", (8, TS, 8, 128))
    k1p = cat("k1p", (16, S, 16, 64))
    v1p = cat("v1p", (16, S, 16, 64))
    k1s = stack("k1s", (8, TS, 16, 64))
    v1s = stack("v1s", (8, TS, 16, 64))
    return (y_prompt, y_sample, k0p, v0p, k0s, v0s, k1p, v1p, k1s, v1s)
```

```python
import numpy as np
import ml_dtypes
from contextlib import ExitStack
import concourse.bass as bass
import concourse.mybir as mybir
from concourse.bass_utils import run_bass_kernel_spmd

F32 = mybir.dt.float32
BF16 = mybir.dt.bfloat16
AF = mybir.ActivationFunctionType
ALU = mybir.AluOpType
AX = mybir.AxisListType

NCORES = 8
D = 1024
S = 2048
PAST = 2048
TS = 16
NG = 8
EPS = 1e-6
LAM_INIT0 = 0.2
NEGBIG = -30000.0
OPT_PJ2 = False
OPT_SUMO = False

C_ID = 0
C_BO = 128
C_ON = 256
C_NT = 384
C_NEG1 = 512
C_ONE = 640
C_BD = 768
C_BDS = C_BD + 8 * 128
C_M1 = C_BDS + 8 * 16
C_M2 = C_M1 + 512
C_TRI = C_M2 + 512
NC16 = C_TRI + 32
F_ID = 0
F_BT = 128
F_BS = 256
NF32 = 384


def _host_constants():
    c16 = np.zeros((128, NC16), np.float32)
    c16[:, C_ID:C_ID + 128] = np.eye(128)
    bo = np.zeros((128, 128), np.float32)
    bo[:64, :64] = 1.0 / 64
    bo[64:, 64:] = 1.0 / 64
    c16[:, C_BO:C_BO + 128] = bo
    c16[:, C_ON:C_ON + 128] = 1.0 / 128
    j = np.arange(128)[:, None]
    i = np.arange(128)[None, :]
    c16[:, C_NT:C_NT + 128] = np.where(j >= i, -1.0, 0.0)
    c16[:, C_NEG1:C_NEG1 + 128] = -1.0
    c16[:, C_ONE:C_ONE + 128] = 1.0
    slopes = np.array([2.0 ** -(h + 1) for h in range(8)], np.float64)
    vis = ~((j >= 64) & (i < 64))
    for h in range(8):
        bd = slopes[h] * (-np.abs(i - j) + (i - 64))
        c16[:, C_BD + h * 128:C_BD + (h + 1) * 128] = np.where(vis, bd, NEGBIG)
    j16 = np.arange(16)[:, None]
    i16 = np.arange(16)[None, :]
    for h in range(8):
        c16[:16, C_BDS + h * 16:C_BDS + (h + 1) * 16] = slopes[h] * (-np.abs(i16 - j16) + (i16 - 8))
    tri = (j < i).astype(np.float32)
    m1 = np.concatenate([np.zeros((128, 128), np.float32), tri], axis=1)
    m2 = np.concatenate([tri, np.ones((128, 128), np.float32)], axis=1)
    c16[:, C_M1:C_M1 + 512] = (np.concatenate([m1, m1], axis=1) - 1.0) * (-NEGBIG)
    c16[:, C_M2:C_M2 + 512] = (np.concatenate([m2, m2], axis=1) - 1.0) * (-NEGBIG)
    tri16 = (j16 < i16).astype(np.float32)
    c16[:16, C_TRI:C_TRI + 32] = (np.concatenate([tri16, tri16], axis=1) - 1.0) * (-NEGBIG)
    cf = np.zeros((128, NF32), np.float32)
    cf[:, F_ID:F_ID + 128] = np.eye(128)
    p = np.arange(128)[:, None]
    for h in range(8):
        for d in range(16):
            cf[:, F_BT + h * 16 + d] = (slopes[h] * (p[:, 0] - 64 - 128 * d))
            cf[:, F_BS + h * 16 + d] = (slopes[h] * (d * 128 + p[:, 0] - (PAST + 8)))
    return c16.astype(ml_dtypes.bfloat16), cf.astype(np.float32)


class Slot:
    __slots__ = ("name", "last_w", "readers", "dma_sem", "dma_cnt")

    def __init__(self, name):
        self.name = name
        self.last_w = None
        self.readers = {}
        self.dma_sem = None
        self.dma_cnt = 0


class Prog:
    ENGS = ("pe", "act", "dve", "pool", "sp")

    def __init__(self, nc, es):
        self.nc = nc
        self.es = es
        self.sem = {e: es.enter_context(nc.semaphore("s_" + e)) for e in ("pe", "act", "dve", "pool")}
        self.cnt = {e: 0 for e in self.ENGS}
        self.lists = {e: [] for e in self.ENGS}
        self.waited = {e: {} for e in self.ENGS}
        self.dma_slots = {}
        self.nslot = 0

    def slot(self, name):
        self.nslot += 1
        return Slot("%s_%d" % (name, self.nslot))

    def _semof(self, key):
        if isinstance(key, str):
            return self.sem[key]
        return self.dma_slots[key[1]].dma_sem

    def _deps(self, eng, reads, writes):
        deps = {}

        def add(ev, raw):
            if ev is None:
                return
            key, val = ev
            if key == eng and eng == "pe":
                return
            if deps.get(key, 0) < val:
                deps[key] = val

        for s in reads:
            add(s.last_w, True)
        for s in writes:
            add(s.last_w, False)
            for ev in s.readers.values():
                add(ev, False)
        waits = []
        w = self.waited[eng]
        for key, val in deps.items():
            if w.get(key, 0) < val:
                w[key] = val
                waits.append((key, val))
        return waits

    def op(self, eng, fn, reads=(), writes=()):
        waits = self._deps(eng, reads, writes)
        self.cnt[eng] += 1
        ev = (eng, self.cnt[eng])
        self.lists[eng].append((waits, fn))
        for s in writes:
            s.last_w = ev
            s.readers = {}
        for s in reads:
            if s.last_w is not ev:
                s.readers[eng] = ev

    def dma(self, q, out, in_, reads=(), writes=()):
        waits = self._deps(q, reads, writes)
        slot = (list(writes) + list(reads))[0]
        if slot.dma_sem is None:
            slot.dma_sem = self.es.enter_context(self.nc.semaphore("d_" + slot.name))
            self.dma_slots[slot.name] = slot
        slot.dma_cnt += 1
        ev = (("dma", slot.name), 16 * slot.dma_cnt)
        self.lists[q].append((waits, ("dma", out, in_, slot)))
        for s in writes:
            s.last_w = ev
            s.readers = {}
        for s in reads:
            s.readers["dma"] = ev

    def finalize(self):
        waits = []
        for slot in self.dma_slots.values():
            key = ("dma", slot.name)
            val = 16 * slot.dma_cnt
            if self.waited["sp"].get(key, 0) < val:
                waits.append((key, val))
        self.lists["sp"].append((waits, None))

    def emit(self, block):
        engmap = {"pe": block.tensor, "act": block.scalar, "dve": block.vector,
                  "pool": block.gpsimd, "sp": block.sync}
        for e in self.ENGS:
            items = self.lists[e]

            def body(eng, items=items, e=e):
                for waits, fn in items:
                    for key, val in waits:
                        eng.wait_ge(self._semof(key), val)
                    if fn is None:
                        continue
                    if isinstance(fn, tuple):
                        _, out, in_, slot = fn
                        eng.dma_start(out=out, in_=in_).then_inc(slot.dma_sem, 16)
                    else:
                        fn(eng).then_inc(self.sem[e], 1)

            engmap[e](body)


def build_program():
    nc = bass.Bass("TRN2", target_bir_lowering=False)

    def din(name, shape, dt=F32):
        return nc.dram_tensor(name, list(shape), dt, kind="ExternalInput").ap()

    def dout(name, shape):
        return nc.dram_tensor(name, list(shape), F32, kind="ExternalOutput").ap()

    xp = din("xp", [2, S, D])
    xs = din("xs", [TS, D])
    ck = [din("ck0", [PAST, D]), din("ck1", [PAST, D])]
    cv = [din("cv0", [PAST, D]), din("cv1", [PAST, D])]
    win = [din("win0", [NG, D, 512]), din("win1", [NG, D, 512])]
    wout = [din("wout0", [D, D]), din("wout1", [D, D])]
    gtd = [din("gt0", [128, D]), din("gt1", [128, D])]
    colsd = din("cols", [128, 4])
    lamd = din("lamv", [128, 4, 64])
    c16d = din("c16", [128, NC16], BF16)
    cfd = din("cf32", [128, NF32])

    yp = dout("yp", [2, S, D])
    ys = dout("ys", [TS, D])
    kop = [dout("k0p", [2, S, D]), dout("k1p", [2, S, D])]
    vop = [dout("v0p", [2, S, D]), dout("v1p", [2, S, D])]
    kos = [dout("k0s", [TS, D]), dout("k1s", [TS, D])]
    vos = [dout("v0s", [TS, D]), dout("v1s", [TS, D])]

    with ExitStack() as es:
        P = Prog(nc, es)

        def sb(name, shape, dt):
            return es.enter_context(nc.sbuf_tensor("sb_" + name, list(shape), dt))

        x_sb = sb("x_sb", [128, 16, D], F32)
        X = [P.slot("x%d" % t) for t in range(16)]
        hT = sb("hT", [128, 8, S], BF16)
        HT = P.slot("hT")
        qpad = [[sb("qpad%d_%d" % (s_, c), [128, S], BF16) for c in range(2)] for s_ in range(2)]
        QP = [[P.slot("qpad%d_%d" % (s_, b)) for b in range(4)] for s_ in range(2)]
        kT = [sb("kT%d" % s_, [128, PAST + TS], BF16) for s_ in range(2)]
        KT = [[P.slot("kT%d_%d" % (s_, b)) for b in range(5)] for s_ in range(2)]
        Vb = [sb("Vb%d" % s_, [128, 17, 128], BF16) for s_ in range(2)]
        VS = [[P.slot("V%d_%d" % (s_, b)) for b in range(5)] for s_ in range(2)]
        gT = [sb("gT%d" % s_, [128, S], BF16) for s_ in range(2)]
        GT = [[P.slot("gT%d_%d" % (s_, b)) for b in range(4)] for s_ in range(2)]
        wi = [sb("wi0", [128, 8, 512], BF16), sb("wi1", [128, 8, 512], BF16)]
        WI = [P.slot("wi0"), P.slot("wi1")]
        wo = [sb("wo%d" % i, [128, D], BF16) for i in range(3)]
        WO = [P.slot("wo%d" % i) for i in range(3)]
        c16 = sb("c16", [128, NC16], BF16)
        cf = sb("cf", [128, NF32], F32)
        cols = sb("cols", [128, 4], F32)
        lamv = sb("lamv", [128, 4, 64], F32)
        CONST = P.slot("const")
        sc = sb("sc", [128, 16], F32)
        SC = P.slot("sc")
        ss = sb("ss", [128, 4], F32)
        SS = P.slot("ss")
        NW32, NW16 = 10, 10
        w32all = sb("w32all", [128, NW32, 512], F32)
        w32 = [w32all[:, i, :] for i in range(NW32)]
        W32 = [P.slot("w32_%d" % i) for i in range(NW32)]
        w16 = [sb("w16_%d" % i, [128, 512], BF16) for i in range(NW16)]
        W16 = [P.slot("w16_%d" % i) for i in range(NW16)]
        xs32 = w32all[:, 6:8, :].rearrange("p a b -> p (a b)")
        XS = [W32[6], W32[7]]
        gtile = w32all[:, 8:10, :].rearrange("p a b -> p (a b)")
        GTL = [W32[8], W32[9]]
        NST = 3
        stg = [sb("stg%d" % i, [128, 512], F32) for i in range(NST)]
        STG = [P.slot("stg%d" % i) for i in range(NST)]
        ps = [es.enter_context(nc.psum_tensor("ps%d" % i, [128, 512], F32)) for i in range(8)]
        PS = [P.slot("ps%d" % i) for i in range(8)]
        PJ, AUX = 6, 7

        ident16 = c16[:, C_ID:C_ID + 128]
        identf = cf[:, F_ID:F_ID + 128]
        NEGLAM, GQ8, GK, GSUB = 0, 1, 2, 3
        IK = 6
        ISQ = (8, 9)

        cnt = {"stg": 0}
        from collections import deque
        fg = deque()
        bg = deque()

        P.dma("sp", c16[:], c16d[:, :], writes=[CONST])
        P.dma("sp", cf[:], cfd[:, :], writes=[CONST])
        P.dma("sp", cols[:], colsd[:, :], writes=[CONST])
        P.dma("sp", lamv[:], lamd[:, :, :], writes=[CONST])
        for s_ in range(2):
            for c in range(2):
                P.op("pool", lambda e, s_=s_, c=c: e.memset(qpad[s_][c][:], 0.0), writes=QP[s_])
        P.op("dve", lambda e: e.tensor_tensor(out=w32[0][:, 0:64], in0=lamv[:, 0, :], in1=lamv[:, 1, :], op=ALU.mult),
             reads=[CONST], writes=[W32[0]])
        P.op("dve", lambda e: e.tensor_tensor(out=w32[0][:, 64:128], in0=lamv[:, 2, :], in1=lamv[:, 3, :], op=ALU.mult),
             reads=[CONST], writes=[W32[0]])
        P.op("dve", lambda e: e.tensor_reduce(out=sc[:, 4:6], in_=w32[0][:, 0:128].rearrange("p (a b) -> p a b", b=64),
                                              axis=AX.X, op=ALU.add), reads=[W32[0]], writes=[SC])
        P.op("act", lambda e: e.activation(out=sc[:, 6:8], in_=sc[:, 4:6], func=AF.Exp), reads=[SC], writes=[SC])
        P.op("dve", lambda e: e.tensor_tensor(out=sc[:, 8:9], in0=sc[:, 7:8], in1=sc[:, 6:7], op=ALU.subtract),
             reads=[SC], writes=[SC])
        P.op("dve", lambda e: e.tensor_scalar(out=sc[:, NEGLAM:NEGLAM + 1], in0=sc[:, 8:9], scalar1=-LAM_INIT0,
                                              scalar2=None, op0=ALU.add), reads=[SC], writes=[SC])
        P.op("dve", lambda e: e.tensor_scalar(out=sc[:, GQ8:GQ8 + 1], in0=cols[:, 0:1], scalar1=0.125, scalar2=None,
                                              op0=ALU.mult), reads=[CONST, SC], writes=[SC])
        P.op("dve", lambda e: e.tensor_copy(out=sc[:, GK:GK + 1], in_=cols[:, 1:2]), reads=[CONST, SC], writes=[SC])
        P.op("dve", lambda e: e.tensor_scalar(out=sc[:, GSUB:GSUB + 1], in0=cols[:, 2:3], scalar1=1.0 - LAM_INIT0,
                                              scalar2=None, op0=ALU.mult), reads=[CONST, SC], writes=[SC])

        class Seq:
            pass

        seqs = []
        for b in range(2):
            q = Seq()
            q.sample = False
            q.T = S
            q.ntile = 16
            q.xsrc = xp[b]
            q.yout = yp[b]
            q.kout = [kop[0][b], kop[1][b]]
            q.vout = [vop[0][b], vop[1][b]]
            q.koff = 0
            q.kt0 = 0
            seqs.append(q)
        q = Seq()
        q.sample = True
        q.T = TS
        q.ntile = 1
        q.xsrc = xs
        q.yout = ys
        q.kout = kos
        q.vout = vos
        q.koff = PAST
        q.kt0 = 16
        seqs.append(q)

        def tp(seq, t):
            return 16 if seq.sample else 128

        work = [(si, layer, g) for si in range(len(seqs)) for layer in range(2) for g in range(NG)]

        def load_w(idx):
            if idx >= len(work):
                return
            si, layer, g = work[idx]
            P.dma("pool", wi[idx % 2][:], win[layer][g].rearrange("(kc p) n -> p kc n", p=128), writes=[WI[idx % 2]])
            P.dma("pool", wo[idx % 3][:], wout[layer][g * 128:(g + 1) * 128, :], writes=[WO[idx % 3]])

        def recip(out, in_, reads, writes, bias=0.0):
            P.op("act", lambda e: e.activation(out=out, in_=in_, func=AF.Ln, bias=bias), reads=reads, writes=writes)
            P.op("act", lambda e: e.activation(out=out, in_=out, func=AF.Exp, scale=-1.0), reads=writes, writes=writes)

        def phase_a(seq, layer):
            P.dma("sp", gtile[:, :], gtd[layer][:, :], writes=GTL)
            for t in range(seq.ntile):
                p_ = tp(seq, t)
                if layer == 0:
                    P.dma("sp", x_sb[:p_, t, :], seq.xsrc[t * 128:t * 128 + p_, :], writes=[X[t]])
                P.op("act", lambda e, t=t, p_=p_: e.activation(out=xs32[:p_, :], in_=x_sb[:p_, t, :], func=AF.Square,
                                                              accum_out=ss[:p_, 0:1]),
                     reads=[X[t]], writes=XS + [SS])
                P.op("act", lambda e, p_=p_: e.activation(out=ss[:p_, 1:2], in_=ss[:p_, 0:1], func=AF.Ln,
                                                          scale=1.0 / D, bias=EPS), reads=[SS], writes=[SS])
                P.op("act", lambda e, p_=p_: e.activation(out=ss[:p_, 2:3], in_=ss[:p_, 1:2], func=AF.Exp, scale=-0.5),
                     reads=[SS], writes=[SS])
                P.op("dve", lambda e, t=t, p_=p_: e.scalar_tensor_tensor(
                    out=xs32[:p_, :], in0=x_sb[:p_, t, :], scalar=ss[:p_, 2:3], in1=gtile[:p_, :],
                    op0=ALU.mult, op1=ALU.mult), reads=[X[t], SS] + GTL, writes=XS)
                for hb in range(2):
                    bank = 6 + hb
                    for k4 in range(4):
                        kc = hb * 4 + k4
                        P.op("pe", lambda e, bank=bank, k4=k4, kc=kc, p_=p_: e.transpose(
                            ps[bank][:, k4 * 128:k4 * 128 + p_], xs32[:p_, kc * 128:(kc + 1) * 128],
                            identf[:p_, :p_]), reads=XS + [CONST], writes=[PS[bank]])
                    src = ps[bank][:, :].rearrange("p (a b) -> p a b", b=128)[:, :, :p_]
                    dst = hT[:, hb * 4:hb * 4 + 4, t * 128:t * 128 + p_]
                    if hb == 0:
                        P.op("act", lambda e, src=src, dst=dst: e.activation(out=dst, in_=src, func=AF.Copy),
                             reads=[PS[bank]], writes=[HT])
                    else:
                        P.op("dve", lambda e, src=src, dst=dst: e.tensor_copy(out=dst, in_=src),
                             reads=[PS[bank]], writes=[HT])

        def rstd_from_ms(bank_ms, nt, l_i, r_i):
            P.op("act", lambda e: e.activation(out=w32[l_i][:, :nt], in_=ps[bank_ms][:, :nt], func=AF.Ln, bias=EPS),
                 reads=[PS[bank_ms]], writes=[W32[l_i]])
            P.op("act", lambda e: e.activation(out=w32[r_i][:, :nt], in_=w32[l_i][:, :nt], func=AF.Exp, scale=-0.5),
                 reads=[W32[l_i]], writes=[W32[r_i]])

        def in_proj_tasks(seq, layer, g, widx):
            st_ = widx % 2
            wb = wi[widx % 2]
            WB = WI[widx % 2]
            T = seq.T
            tasks = []
            qp, kTs, Vs, gTs = qpad[st_], kT[st_], Vb[st_], gT[st_]
            AUX = 7
            TRB = 6

            def next_stg():
                i = cnt["stg"] % NST
                cnt["stg"] += 1
                return i

            if seq.sample:
                ckv = ck[layer].rearrange("(t p) f -> p t f", p=128)
                cvv = cv[layer].rearrange("(t p) f -> p t f", p=128)
                tasks.append(lambda: P.dma("pool", Vs[:, 0:16, :], cvv[:, :, g * 128:(g + 1) * 128], writes=VS[st_][0:4]))

                def cache_chunk(c4):
                    sidx = next_stg()
                    P.dma("sp", stg[sidx][:, :].rearrange("p (a b) -> p a b", b=128),
                          ckv[:, c4 * 4:c4 * 4 + 4, g * 128:(g + 1) * 128], writes=[STG[sidx]])
                    tb_ = 6 + c4 % 2
                    for i4 in range(4):
                        P.op("pe", lambda e, i4=i4: e.transpose(
                            ps[tb_][:, i4 * 128:(i4 + 1) * 128], stg[sidx][:, i4 * 128:(i4 + 1) * 128], identf[:, :]),
                            reads=[STG[sidx], CONST], writes=[PS[tb_]])
                    P.op("dve", lambda e: e.tensor_copy(out=kTs[:, c4 * 512:(c4 + 1) * 512], in_=ps[tb_][:, :]),
                         reads=[PS[tb_]], writes=[KT[st_][c4]])
                for c4 in range(4):
                    tasks.append(lambda c4=c4: cache_chunk(c4))

            def mm_proj(t0, nt, col0, PJ, k0, k1):
                for kc in range(k0, k1):
                    P.op("pe", lambda e, kc=kc: e.matmul(
                        ps[PJ][:, :nt], wb[:, kc, col0:col0 + 128], hT[:, kc, t0:t0 + nt],
                        start=(kc == 0), stop=(kc == 7)), reads=[WB, HT], writes=[PS[PJ]])

            def gate_a(t0, nt, PJ, IREC):
                P.op("act", lambda e: e.activation(out=w32[IREC][:, :nt], in_=ps[PJ][:, :nt], func=AF.Exp, scale=-1.0),
                     reads=[PS[PJ]], writes=[W32[IREC]])
                P.op("act", lambda e: e.activation(out=w32[IREC][:, :nt], in_=w32[IREC][:, :nt], func=AF.Ln, bias=1.0),
                     reads=[W32[IREC]], writes=[W32[IREC]])
                P.op("act", lambda e: e.activation(out=w32[IREC][:, :nt], in_=w32[IREC][:, :nt], func=AF.Exp,
                                                   scale=-1.0), reads=[W32[IREC]], writes=[W32[IREC]])

            def gate_c(t0, nt, PJ, IREC):
                P.op("dve", lambda e: e.tensor_tensor(out=gTs[:, t0:t0 + nt], in0=ps[PJ][:, :nt], in1=w32[IREC][:, :nt],
                                                      op=ALU.mult), reads=[PS[PJ], W32[IREC]], writes=[GT[st_][t0 // 512]])

            def norm_sq(nt, sqi, PJ):
                P.op("act", lambda e: e.activation(out=w16[sqi][:, :nt], in_=ps[PJ][:, :nt], func=AF.Square),
                     reads=[PS[PJ]], writes=[W16[sqi]])

            def norm_ms(nt, sqi, MS):
                P.op("pe", lambda e: e.matmul(ps[MS][:, :nt], c16[:, C_BO:C_BO + 128], w16[sqi][:, :nt],
                                              start=True, stop=True), reads=[W16[sqi], CONST], writes=[PS[MS]])

            def norm_b(nt, MS, il):
                P.op("act", lambda e: e.activation(out=w32[il][:, :nt], in_=ps[MS][:, :nt], func=AF.Ln, bias=EPS),
                     reads=[PS[MS]], writes=[W32[il]])

            def norm_c(nt, il):
                P.op("act", lambda e: e.activation(out=w32[il][:, :nt], in_=w32[il][:, :nt], func=AF.Exp, scale=-0.5),
                     reads=[W32[il]], writes=[W32[il]])

            def q0_c(t0, nt, PJ, ir):
                for c in range(2):
                    r0, r1 = c * 64, (c + 1) * 64
                    P.op("dve", lambda e, c=c, r0=r0, r1=r1: e.scalar_tensor_tensor(
                        out=qp[c][r0:r1, t0:t0 + nt], in0=ps[PJ][r0:r1, :nt], scalar=sc[r0:r1, GQ8:GQ8 + 1],
                        in1=w32[ir][r0:r1, :nt], op0=ALU.mult, op1=ALU.mult),
                        reads=[PS[PJ], SC, W32[ir]], writes=[QP[st_][t0 // 512]])

            def k0_c1(t0, nt, PJ, ir):
                P.op("dve", lambda e: e.scalar_tensor_tensor(
                    out=w32[IK][:, :nt], in0=ps[PJ][:, :nt], scalar=sc[:, GK:GK + 1], in1=w32[ir][:, :nt],
                    op0=ALU.mult, op1=ALU.mult), reads=[PS[PJ], SC, W32[ir]], writes=[W32[IK]])

            def k0_c(t0, nt, tk):
                P.op("act", lambda e: e.activation(out=kTs[:, tk:tk + nt], in_=w32[IK][:, :nt], func=AF.Copy),
                     reads=[W32[IK]], writes=[KT[st_][tk // 512]])
                ntile = (nt + 127) // 128
                p_ = min(nt, 128)
                for i4 in range(ntile):
                    P.op("pe", lambda e, i4=i4: e.transpose(
                        ps[TRB][:p_, i4 * 128:(i4 + 1) * 128], w32[IK][:, i4 * 128:i4 * 128 + p_], identf[:, :]),
                        reads=[W32[IK], CONST], writes=[PS[TRB]])

            def k0_d(t0, nt):
                ntile = (nt + 127) // 128
                p_ = min(nt, 128)
                si_ = next_stg()
                P.op("dve", lambda e: e.tensor_copy(out=stg[si_][:p_, :ntile * 128], in_=ps[TRB][:p_, :ntile * 128]),
                     reads=[PS[TRB]], writes=[STG[si_]])
                if seq.sample:
                    dst = seq.kout[layer][0:p_, g * 128:(g + 1) * 128]
                    srcv = stg[si_][:p_, 0:128]
                else:
                    dst = seq.kout[layer].rearrange("(n p) f -> p n f", p=128)[
                        :, t0 // 128:t0 // 128 + ntile, g * 128:(g + 1) * 128]
                    srcv = stg[si_][:, :ntile * 128].rearrange("p (n f) -> p n f", f=128)
                P.dma("sp", dst, srcv, reads=[STG[si_]])

            def q1_b(t0, nt, PJ):
                for c in range(2):
                    r0, r1 = c * 64, (c + 1) * 64
                    P.op("dve", lambda e, c=c, r0=r0, r1=r1: e.tensor_scalar(
                        out=qp[c][r0:r1, t0:t0 + nt], in0=ps[PJ][r0:r1, :nt], scalar1=0.125, scalar2=None,
                        op0=ALU.mult), reads=[PS[PJ]], writes=[QP[st_][t0 // 512]])

            def k1_b(t0, nt, tk, PJ):
                P.op("dve", lambda e: e.tensor_copy(out=kTs[:, tk:tk + nt], in_=ps[PJ][:, :nt]),
                     reads=[PS[PJ]], writes=[KT[st_][tk // 512]])

            def v0_mm(t0, i4, p_):
                for kc in range(8):
                    P.op("pe", lambda e, kc=kc: e.matmul(
                        ps[AUX][:p_, i4 * 128:(i4 + 1) * 128], hT[:, kc, t0 + i4 * 128:t0 + i4 * 128 + p_],
                        wb[:, kc, 256:384], start=(kc == 0), stop=(kc == 7)),
                        reads=[WB, HT], writes=[PS[AUX]])

            def v0_evac(t0, nt):
                ntile = (nt + 127) // 128
                p_ = min(nt, 128)
                si_ = next_stg()
                P.op("dve", lambda e: e.tensor_copy(out=stg[si_][:p_, :ntile * 128], in_=ps[AUX][:p_, :ntile * 128]),
                     reads=[PS[AUX]], writes=[STG[si_]])
                kt_ = seq.kt0 + t0 // 128
                P.op("pool", lambda e: e.tensor_copy(
                    out=Vs[:p_, kt_:kt_ + ntile, :],
                    in_=stg[si_][:p_, :ntile * 128].rearrange("p (n f) -> p n f", f=128)),
                    reads=[STG[si_]], writes=[VS[st_][kt_ // 4]])
                if seq.sample:
                    dst = seq.vout[layer][0:p_, g * 128:(g + 1) * 128]
                    srcv = stg[si_][:p_, 0:128]
                else:
                    dst = seq.vout[layer].rearrange("(n p) f -> p n f", p=128)[
                        :, t0 // 128:t0 // 128 + ntile, g * 128:(g + 1) * 128]
                    srcv = stg[si_][:, :ntile * 128].rearrange("p (n f) -> p n f", f=128)
                P.dma("sp", dst, srcv, reads=[STG[si_]])

            def kv1_mm(t0, i4, j, p_, KB):
                for kc in range(8):
                    P.op("pe", lambda e, kc=kc: e.matmul(
                        ps[KB][:p_, j * 256:(j + 1) * 256], hT[:, kc, t0 + i4 * 128:t0 + i4 * 128 + p_],
                        wb[:, kc, 128:384], start=(kc == 0), stop=(kc == 7)),
                        reads=[WB, HT], writes=[PS[KB]])

            def kv1_evac(t0, i2, n2, p_, KB):
                si_ = next_stg()
                P.op("dve", lambda e: e.tensor_copy(out=stg[si_][:p_, :n2 * 256], in_=ps[KB][:p_, :n2 * 256]),
                     reads=[PS[KB]], writes=[STG[si_]])
                kt_ = seq.kt0 + t0 // 128 + i2
                sview = stg[si_][:p_, :n2 * 256].rearrange("p (n f) -> p n f", f=256)
                P.op("pool", lambda e: e.tensor_copy(out=Vs[:p_, kt_:kt_ + n2, :], in_=sview[:, :, 128:256]),
                     reads=[STG[si_]], writes=[VS[st_][kt_ // 4]])
                if seq.sample:
                    P.dma("sp", seq.kout[layer][0:p_, g * 128:(g + 1) * 128], stg[si_][:p_, 0:128], reads=[STG[si_]])
                    P.dma("sp", seq.vout[layer][0:p_, g * 128:(g + 1) * 128], stg[si_][:p_, 128:256],
                          reads=[STG[si_]])
                else:
                    n0 = t0 // 128 + i2
                    kd = seq.kout[layer].rearrange("(n p) f -> p n f", p=128)[:, n0:n0 + n2, g * 128:(g + 1) * 128]
                    vd = seq.vout[layer].rearrange("(n p) f -> p n f", p=128)[:, n0:n0 + n2, g * 128:(g + 1) * 128]
                    P.dma("sp", kd, sview[:, :, 0:128], reads=[STG[si_]])
                    P.dma("sp", vd, sview[:, :, 128:256], reads=[STG[si_]])

            def A(lst, f, *a):
                lst.append(lambda: f(*a))

            MSB = 6
            blocks = []
            for bi_, t0 in enumerate(range(0, T, 512)):
                nt = min(512, T - t0)
                blocks.append(dict(t0=t0, nt=nt, tk=seq.koff + t0, ntile=(nt + 127) // 128, p_=min(nt, 128),
                                   par=bi_ % 2))

            def banks(bk):
                par = bk["par"]
                return (3 * par, 3 * par + 1, 3 * par + 2), (3 * par, 3 * par + 1, 3 * par + 2)

            def emit_block(bk, prev):
                t0, nt, tk, ntile, p_ = bk["t0"], bk["nt"], bk["tk"], bk["ntile"], bk["p_"]
                (pq, pk_, pg), (il, ir, irec) = banks(bk)
                A(tasks, mm_proj, t0, nt, 0, pq, 0, 8)
                A(tasks, mm_proj, t0, nt, 128, pk_, 0, 8)
                if prev is not None:
                    tail_dve(prev)
                if layer == 0:
                    A(tasks, norm_sq, nt, ISQ[0], pq)
                    A(tasks, norm_ms, nt, ISQ[0], MSB)
                    A(tasks, norm_b, nt, MSB, il)
                A(tasks, mm_proj, t0, nt, 384, pg, 0, 8)
                if layer == 0:
                    for i4 in range(ntile):
                        A(tasks, v0_mm, t0, i4, p_)
                    if prev is not None:
                        A(tasks, k0_c, prev["t0"], prev["nt"], prev["tk"])
                        A(tasks, k0_d, prev["t0"], prev["nt"])
                    A(tasks, norm_sq, nt, ISQ[1], pk_)
                    A(tasks, gate_a, t0, nt, pg, irec)
                    A(tasks, norm_ms, nt, ISQ[1], MSB)
                    A(tasks, norm_b, nt, MSB, ir)
                    A(tasks, norm_c, nt, il)
                    A(tasks, norm_c, nt, ir)
                    A(tasks, v0_evac, t0, nt)
                else:
                    kbs = []
                    for i2 in range(0, ntile, 2):
                        n2 = min(2, ntile - i2)
                        kb = 6 + (i2 // 2) % 2
                        for j in range(n2):
                            A(tasks, kv1_mm, t0, i2 + j, j, p_, kb)
                        kbs.append((i2, n2, kb))
                    A(tasks, gate_a, t0, nt, pg, irec)
                    for (i2, n2, kb) in kbs:
                        A(tasks, kv1_evac, t0, i2, n2, p_, kb)

            def tail_dve(bk):
                t0, nt, tk = bk["t0"], bk["nt"], bk["tk"]
                (pq, pk_, pg), (il, ir, irec) = banks(bk)
                if layer == 0:
                    A(tasks, q0_c, t0, nt, pq, il)
                    A(tasks, k0_c1, t0, nt, pk_, ir)
                else:
                    A(tasks, q1_b, t0, nt, pq)
                    A(tasks, k1_b, t0, nt, tk, pk_)
                A(tasks, gate_c, t0, nt, pg, irec)

            prev = None
            for bk in blocks:
                emit_block(bk, prev)
                prev = bk
            tail_dve(prev)
            if layer == 0:
                A(tasks, k0_c, prev["t0"], prev["nt"], prev["tk"])
                A(tasks, k0_d, prev["t0"], prev["nt"])
            return tasks

        def out_proj_member(widx, otile, tq, c0, nq, yb):
            for half in range(2):
                P.op("pe", lambda e, half=half: e.matmul(
                    ps[yb][:nq, :], w16[otile][:, c0:c0 + nq], wo[widx % 3][:, half * 512:(half + 1) * 512],
                    start=True, stop=True), reads=[W16[otile], WO[widx % 3]], writes=[PS[yb]])
                P.op("dve", lambda e, half=half: e.tensor_tensor(
                    out=x_sb[:nq, tq, half * 512:(half + 1) * 512], in0=ps[yb][:nq, :],
                    in1=x_sb[:nq, tq, half * 512:(half + 1) * 512], op=ALU.add),
                    reads=[PS[yb], X[tq]], writes=[X[tq]])

        def out_proj_mm(widx, otile, c0, nq, yb, half):
            P.op("pe", lambda e: e.matmul(
                ps[yb][:nq, :], w16[otile][:, c0:c0 + nq], wo[widx % 3][:, half * 512:(half + 1) * 512],
                start=True, stop=True), reads=[W16[otile], WO[widx % 3]], writes=[PS[yb]])

        def out_proj_add(tq, nq, yb, half):
            P.op("dve", lambda e: e.tensor_tensor(
                out=x_sb[:nq, tq, half * 512:(half + 1) * 512], in0=ps[yb][:nq, :],
                in1=x_sb[:nq, tq, half * 512:(half + 1) * 512], op=ALU.add),
                reads=[PS[yb], X[tq]], writes=[X[tq]])

        def queue_out_proj(widx, ot, members, yb, dly, tag):
            for (tq, c0, nq) in members:
                for half in range(2):
                    fg_add(dly, lambda c0=c0, nq=nq, half=half: out_proj_mm(widx, ot, c0, nq, yb, half), tag)
                    fg_add(dly + 1, lambda tq=tq, nq=nq, half=half: out_proj_add(tq, nq, yb, half), tag)
                    dly += 2

        gtick = [0]

        def fg_add(delay, fn, tag=None):
            due = gtick[0] + delay
            if fg and fg[-1][0] > due:
                due = fg[-1][0]
            fg.append((due, fn, tag))

        def flush_fg(tag=None):
            if tag is not None and not any(t == tag for (_, _, t) in fg):
                return
            while fg:
                _, fn, t = fg.popleft()
                fn()
                if tag is not None and not any(t2 == tag for (_, _, t2) in fg):
                    break

        pq = [deque(), deque()]

        def pq_add(par, delay, fn):
            due = gtick[0] + delay
            if pq[par] and pq[par][-1][0] > due:
                due = pq[par][-1][0]
            pq[par].append((due, fn))

        def pq_flush(par):
            while pq[par]:
                pq[par].popleft()[1]()

        hoist = []

        def run_pipeline(steps, stages):
            n = len(steps)
            k = len(stages)
            nticks = n + k - 1
            for tick in range(nticks):
                for si in reversed(range(k)):
                    i = tick - si
                    if 0 <= i < n:
                        stages[si](steps[i], i)
                gtick[0] += 1
                while fg and fg[0][0] <= gtick[0]:
                    fg.popleft()[1]()
                for par in range(2):
                    while pq[par] and pq[par][0][0] <= gtick[0]:
                        pq[par].popleft()[1]()
            while hoist:
                hoist.pop(0)()
            pq_flush(0)
            pq_flush(1)
            flush_fg()

        def attn0(seq, g, widx):
            st_ = widx % 2
            qp, kTs, Vs, gTs = qpad[st_], kT[st_], Vb[st_], gT[st_]
            steps = []
            if not seq.sample:
                for a in range(0, 16, 2):
                    blk = dict(q0=a * 128, members=[(a, 0, 128), (a + 1, 128, 128)], nq=128, nm=2,
                               ob=[(2, 3), (6, 7)][(a // 2) % 2], par=(a // 2) % 2)
                    for d in range(0, a + 2):
                        if d == 0:
                            pairs = [(a, 0), (a + 1, 1)]
                            kind = "diag"
                        elif d <= a:
                            pairs = [(a - d, 0), (a + 1 - d, 1)]
                            kind = "off"
                        else:
                            pairs = [(0, 1)]
                            kind = "off"
                        first = {0: d == 0, 1: d == 0}
                        last = {0: d == a, 1: d == a + 1}
                        steps.append(dict(blk=blk, pairs=pairs, kind=kind, pk=128,
                                          bias=cf[:, F_BT + g * 16 + d:F_BT + g * 16 + d + 1],
                                          first=first, last=last, sfirst=(d == 0), slast=(d == a + 1)))
            else:
                blk = dict(q0=0, members=[(0, 0, 16)], nq=16, nm=1, ob=(2, 3), par=0)
                for kt in range(16):
                    steps.append(dict(blk=blk, pairs=[(kt, 0)], kind="off", pk=128,
                                      bias=cf[:, F_BS + g * 16 + kt:F_BS + g * 16 + kt + 1],
                                      first={0: kt == 0}, last={0: False}, sfirst=(kt == 0), slast=False))
                steps.append(dict(blk=blk, pairs=[(16, 0)], kind="diag", pk=16, bias=None,
                                  first={0: False}, last={0: True}, sfirst=False, slast=True))

            def region(st):
                nq_ = st["blk"]["nq"]
                ms = [m for (_, m) in st["pairs"]]
                return min(ms) * 2 * nq_, (max(ms) + 1) * 2 * nq_

            def s1(st, i):
                sbk = i % 2
                nq_ = st["blk"]["nq"]
                pk = st["pk"]
                q0 = st["blk"]["q0"]
                diag = st["kind"] == "diag"
                if st["sfirst"]:
                    pq_flush(st["blk"]["par"])
                for (kt, m) in st["pairs"]:
                    for c in range(2):
                        col = (m * 2 + c) * nq_
                        qc = slice(q0 + m * nq_, q0 + (m + 1) * nq_)
                        P.op("pe", lambda e, col=col, kt=kt, c=c, qc=qc: e.matmul(
                            ps[sbk][:pk, col:col + nq_], kTs[:, kt * 128:kt * 128 + pk], qp[c][:, qc],
                            start=True, stop=(not diag)), reads=[KT[st_][kt // 4], QP[st_][q0 // 512]], writes=[PS[sbk]])
                        if diag:
                            if pk == 128:
                                bd = c16[:, C_BD + g * 128:C_BD + (g + 1) * 128]
                            else:
                                bd = c16[:16, C_BDS + g * 16:C_BDS + (g + 1) * 16]
                            P.op("pe", lambda e, col=col, bd=bd: e.matmul(
                                ps[sbk][:pk, col:col + nq_], ident16[:pk, :pk], bd, start=False, stop=True),
                                reads=[CONST], writes=[PS[sbk]])

            def s2(st, i):
                sbk = i % 2
                eb = i % 4
                pk = st["pk"]
                r0, r1 = region(st)
                if st["kind"] == "diag":
                    P.op("act", lambda e: e.activation(out=w16[eb][:pk, r0:r1], in_=ps[sbk][:pk, r0:r1], func=AF.Exp),
                         reads=[PS[sbk]], writes=[W16[eb]])
                else:
                    P.op("act", lambda e: e.activation(out=w16[eb][:pk, r0:r1], in_=ps[sbk][:pk, r0:r1], func=AF.Exp,
                                                       bias=st["bias"][:pk, :]),
                         reads=[PS[sbk], CONST], writes=[W16[eb]])

            def s3(st, i):
                eb = i % 4
                pk = st["pk"]
                nq_ = st["blk"]["nq"]
                if not OPT_SUMO:
                    r0, r1 = region(st)
                    for (kt, m) in st["pairs"]:
                        ob = st["blk"]["ob"][m]
                        P.op("pe", lambda e, kt=kt, m=m, ob=ob: e.matmul(
                            ps[ob][:, 0:2 * nq_], Vs[:pk, kt, :], w16[eb][:pk, m * 2 * nq_:(m + 1) * 2 * nq_],
                            start=st["first"][m], stop=st["last"][m]), reads=[VS[st_][kt // 4], W16[eb]], writes=[PS[ob]])
                    sb_ = 4 + st["blk"]["par"]
                    P.op("pe", lambda e: e.matmul(ps[sb_][:, r0:r1], c16[:pk, C_ONE:C_ONE + 128], w16[eb][:pk, r0:r1],
                                                  start=st["sfirst"], stop=st["slast"]),
                         reads=[CONST, W16[eb]], writes=[PS[sb_]])
                    if st["slast"]:
                        epilogue(st["blk"])
                    return
                for (kt, m) in st["pairs"]:
                    ob = 2 + m
                    P.op("pe", lambda e, kt=kt, m=m, ob=ob: e.matmul(
                        ps[ob][:, 0:2 * nq_], Vs[:pk, kt, :], w16[eb][:pk, m * 2 * nq_:(m + 1) * 2 * nq_],
                        start=st["first"][m], stop=False), reads=[VS[st_][kt // 4], W16[eb]], writes=[PS[ob]])
                    P.op("pe", lambda e, m=m, ob=ob: e.matmul(
                        ps[ob][:, 256:256 + 2 * nq_], c16[:pk, C_ONE:C_ONE + 128],
                        w16[eb][:pk, m * 2 * nq_:(m + 1) * 2 * nq_], start=False, stop=st["last"][m]),
                        reads=[CONST, W16[eb]], writes=[PS[ob]])
                if st["slast"]:
                    epilogue(st["blk"])

            def epilogue(blk):
                nq_ = blk["nq"]
                nm = blk["nm"]
                nqt = nq_ * nm
                q0 = blk["q0"]
                par = blk["par"]
                e0, e1, e2, e3, e4 = [par * 5 + j for j in range(5)]
                sqt = 6 + par
                ot = 4 + par
                ob = blk["ob"]
                sbk = 4 + par
                b_ms = ob[0]
                ybank = (ob[1], ob[0])
                pq_flush(par)
                P.op("dve", lambda e: e.tensor_copy(out=w32[e0][:, :2 * nqt], in_=ps[sbk][:, :2 * nqt]),
                     reads=[PS[sbk]], writes=[W32[e0]])
                for m in range(nm):
                    P.op("dve", lambda e, m=m: e.tensor_copy(
                        out=w32[e1][:, m * 2 * nq_:(m + 1) * 2 * nq_], in_=ps[ob[m]][:, 0:2 * nq_]),
                        reads=[PS[ob[m]]], writes=[W32[e1]])

                def p_rs():
                    P.op("act", lambda e: e.activation(out=w32[e0][:, :2 * nqt], in_=w32[e0][:, :2 * nqt], func=AF.Ln),
                         reads=[W32[e0]], writes=[W32[e0]])
                    P.op("act", lambda e: e.activation(out=w32[e0][:, :2 * nqt], in_=w32[e0][:, :2 * nqt], func=AF.Exp,
                                                       scale=-1.0), reads=[W32[e0]], writes=[W32[e0]])

                def p_comb():
                    P.op("dve", lambda e: e.tensor_tensor(out=w32[e1][:, :2 * nqt], in0=w32[e1][:, :2 * nqt],
                                                          in1=w32[e0][:, :2 * nqt], op=ALU.mult),
                         reads=[W32[e1], W32[e0]], writes=[W32[e1]])
                    t4 = w32[e1][:, :2 * nqt].rearrange("p (m c q) -> p m c q", m=nm, c=2)
                    ocv = w32[e2][:, :nqt].rearrange("p (m q) -> p m q", m=nm)
                    P.op("dve", lambda e: e.scalar_tensor_tensor(out=ocv, in0=t4[:, :, 1, :],
                                                                 scalar=sc[:, NEGLAM:NEGLAM + 1],
                                                                 in1=t4[:, :, 0, :], op0=ALU.mult, op1=ALU.add),
                         reads=[W32[e1], SC], writes=[W32[e2]])
                    P.op("dve", lambda e: e.tensor_tensor(out=w16[sqt][:, :nqt], in0=w32[e2][:, :nqt],
                                                          in1=w32[e2][:, :nqt], op=ALU.mult),
                         reads=[W32[e2]], writes=[W16[sqt]])

                def p_ms():
                    P.op("pe", lambda e: e.matmul(ps[b_ms][:, :nqt], c16[:, C_ON:C_ON + 128], w16[sqt][:, :nqt],
                                                  start=True, stop=True), reads=[CONST, W16[sqt]], writes=[PS[b_ms]])

                def p_r():
                    P.op("act", lambda e: e.activation(out=w32[e3][:, :nqt], in_=ps[b_ms][:, :nqt], func=AF.Ln,
                                                       bias=EPS), reads=[PS[b_ms]], writes=[W32[e3]])
                    P.op("act", lambda e: e.activation(out=w32[e3][:, :nqt], in_=w32[e3][:, :nqt], func=AF.Exp,
                                                       scale=-0.5), reads=[W32[e3]], writes=[W32[e3]])

                def p_og():
                    P.op("dve", lambda e: e.scalar_tensor_tensor(out=w32[e4][:, :nqt], in0=w32[e2][:, :nqt],
                                                                 scalar=sc[:, GSUB:GSUB + 1], in1=w32[e3][:, :nqt],
                                                                 op0=ALU.mult, op1=ALU.mult),
                         reads=[W32[e2], SC, W32[e3]], writes=[W32[e4]])
                    P.op("pool", lambda e: e.tensor_tensor(out=w16[ot][:, :nqt], in0=w32[e4][:, :nqt],
                                                           in1=gTs[:, q0:q0 + nqt], op=ALU.mult),
                         reads=[W32[e4], GT[st_][q0 // 512]], writes=[W16[ot]])

                pq_add(par, 2, p_rs)
                pq_add(par, 3, p_comb)
                pq_add(par, 5, p_ms)
                pq_add(par, 6, p_r)
                pq_add(par, 8, p_og)
                dly = 10
                for (tq, c0, nq) in blk["members"]:
                    for half in range(2):
                        yb = ybank[half]
                        pq_add(par, dly + half, lambda c0=c0, nq=nq, half=half, yb=yb: out_proj_mm(
                            widx, ot, c0, nq, yb, half))
                    for half in range(2):
                        yb = ybank[half]
                        pq_add(par, dly + 2 + half, lambda tq=tq, nq=nq, half=half, yb=yb: out_proj_add(
                            tq, nq, yb, half))
                    dly += 4

            run_pipeline(steps, [s1, s2, s3])

        def attn1(seq, g, widx):
            st_ = widx % 2
            qp, kTs, Vs, gTs = qpad[st_], kT[st_], Vb[st_], gT[st_]
            steps = []
            if not seq.sample:
                for a in range(0, 16, 2):
                    blk = dict(q0=a * 128, members=[(a, 0, 128), (a + 1, 128, 128)], nqt=256, bi=a // 2)
                    for j, kt in enumerate(range(a + 1, -1, -1)):
                        mask = None
                        if kt == a + 1:
                            mask = c16[:, C_M1:C_M1 + 512]
                        elif kt == a:
                            mask = c16[:, C_M2:C_M2 + 512]
                        steps.append(dict(blk=blk, kt=kt, pk=128, mask=mask, first=(kt == a + 1), last=(kt == 0), j=j))
            else:
                blk = dict(q0=0, members=[(0, 0, 16)], nqt=16, bi=0)
                steps.append(dict(blk=blk, kt=16, pk=16, mask=c16[:16, C_TRI:C_TRI + 32], first=True, last=False, j=0))
                for j, kt in enumerate(range(15, -1, -1)):
                    steps.append(dict(blk=blk, kt=kt, pk=128, mask=None, first=False, last=(kt == 0), j=j + 1))
            def s1(st, i):
                ab = i % 3
                nqt = st["blk"]["nqt"]
                q0 = st["blk"]["q0"]
                pk = st["pk"]
                kt = st["kt"]
                for h in range(2):
                    P.op("pe", lambda e, h=h: e.matmul(
                        ps[ab][:pk, h * nqt:(h + 1) * nqt], kTs[:, kt * 128:kt * 128 + pk], qp[h][:, q0:q0 + nqt],
                        start=(h == 0), stop=False, skip_group_check=True),
                        reads=[KT[st_][kt // 4], QP[st_][q0 // 512]], writes=[PS[ab]])
                if st["mask"] is not None:
                    P.op("pe", lambda e: e.matmul(ps[ab][:pk, :2 * nqt], ident16[:pk, :pk], st["mask"],
                                                  start=False, stop=False, skip_group_check=True),
                         reads=[CONST], writes=[PS[ab]])

            def s2(st, i):
                ab = i % 3
                eb = (0, 1, 6)[i % 3]
                spb = (0, 1, 2, 8)[i % 4]
                pk = st["pk"]
                n2 = 2 * st["blk"]["nqt"]
                P.op("act", lambda e: e.activation(out=w32[eb][:pk, :n2], in_=ps[ab][:pk, :n2], func=AF.Exp),
                     reads=[PS[ab]], writes=[W32[eb]])
                P.op("act", lambda e: e.activation(out=w16[spb][:pk, :n2], in_=w32[eb][:pk, :n2], func=AF.Ln, bias=1.0),
                     reads=[W32[eb]], writes=[W16[spb]])

            def s3(st, i):
                ab = i % 3
                spb = (0, 1, 2, 8)[i % 4]
                CB = (3, 6)[i % 2]
                pk = st["pk"]
                n2 = 2 * st["blk"]["nqt"]
                P.op("pe", lambda e: e.matmul(ps[ab][:pk, :n2], c16[:pk, C_NT:C_NT + pk], w16[spb][:pk, :n2],
                                              start=False, stop=True, skip_group_check=True),
                     reads=[CONST, W16[spb]], writes=[PS[ab]])
                if not st["last"]:
                    P.op("pe", lambda e: e.matmul(ps[CB][:, :n2], c16[:pk, C_NEG1:C_NEG1 + 128], w16[spb][:pk, :n2],
                                                  start=True, stop=True), reads=[CONST, W16[spb]], writes=[PS[CB]])

            def s4(st, i):
                ab = i % 3
                tb = (2, 3, 7)[i % 3]
                ra, rb = 4 + st["j"] % 2, 4 + (st["j"] + 1) % 2
                CB = (3, 6)[i % 2]
                pk = st["pk"]
                n2 = 2 * st["blk"]["nqt"]
                if st["first"]:
                    P.op("dve", lambda e: e.tensor_copy(out=w32[tb][:pk, :n2], in_=ps[ab][:pk, :n2]),
                         reads=[PS[ab]], writes=[W32[tb]])
                else:
                    P.op("dve", lambda e: e.tensor_tensor(out=w32[tb][:pk, :n2], in0=ps[ab][:pk, :n2],
                                                          in1=w32[ra][:pk, :n2], op=ALU.add),
                         reads=[PS[ab], W32[ra]], writes=[W32[tb]])
                if not st["last"]:
                    if st["first"]:
                        P.op("dve", lambda e: e.tensor_copy(out=w32[rb][:, :n2], in_=ps[CB][:, :n2]),
                             reads=[PS[CB]], writes=[W32[rb]])
                    else:
                        P.op("dve", lambda e: e.tensor_tensor(out=w32[rb][:, :n2], in0=ps[CB][:, :n2],
                                                              in1=w32[ra][:, :n2], op=ALU.add),
                             reads=[PS[CB], W32[ra]], writes=[W32[rb]])

            def s5(st, i):
                tb = (2, 3, 7)[i % 3]
                atb = (3, 4, 5, 9)[i % 4]
                pk = st["pk"]
                n2 = 2 * st["blk"]["nqt"]
                P.op("act", lambda e: e.activation(out=w16[atb][:pk, :n2], in_=w32[tb][:pk, :n2], func=AF.Exp),
                     reads=[W32[tb]], writes=[W16[atb]])

            def s6(st, i):
                atb = (3, 4, 5, 9)[i % 4]
                pk = st["pk"]
                nqt = st["blk"]["nqt"]
                n2 = 2 * nqt
                kt = st["kt"]
                ob = 4 + st["blk"]["bi"] % 2
                P.op("pe", lambda e: e.matmul(ps[ob][:, :n2], Vs[:pk, kt, :], w16[atb][:pk, :n2],
                                              start=st["first"], stop=st["last"]),
                     reads=[VS[st_][kt // 4], W16[atb]], writes=[PS[ob]])
                if st["last"]:
                    q0 = st["blk"]["q0"]
                    ot = 6 + st["blk"]["bi"] % 2
                    tag = ("ep1", st["blk"]["bi"] % 2)
                    flush_fg(tag)

                    def p_gate():
                        for h in range(2):
                            r0, r1 = h * 64, (h + 1) * 64
                            P.op("dve", lambda e, h=h, r0=r0, r1=r1: e.tensor_tensor(
                                out=w16[ot][r0:r1, :nqt], in0=ps[ob][r0:r1, h * nqt:(h + 1) * nqt],
                                in1=gTs[r0:r1, q0:q0 + nqt], op=ALU.mult),
                                reads=[PS[ob], GT[st_][q0 // 512]], writes=[W16[ot]])

                    pq_add(st["blk"]["bi"] % 2, 1, p_gate)
                    queue_out_proj(widx, ot, st["blk"]["members"], 7, 4, tag)

            run_pipeline(steps, [s1, s2, s3, s4, s5, s6])

        load_w(0)
        load_w(1)
        widx = 0
        for si, seq in enumerate(seqs):
            for layer in range(2):
                phase_a(seq, layer)
                nxt = None
                for g in range(NG):
                    tl = nxt if nxt is not None else in_proj_tasks(seq, layer, g, widx)
                    for t in tl:
                        t()
                    load_w(widx + 2)
                    nxt = None
                    if g + 1 < NG and not seq.sample:
                        nxt = in_proj_tasks(seq, layer, g + 1, widx + 1)
                        hoist.extend(nxt[:2])
                        nxt = nxt[2:]
                    if layer == 0:
                        attn0(seq, g, widx)
                    else:
                        attn1(seq, g, widx)
                    widx += 1
            for t in range(seq.ntile):
                p_ = tp(seq, t)
                P.dma("sp", seq.yout[t * 128:t * 128 + p_, :], x_sb[:p_, t, :], reads=[X[t]])

        P.finalize()
        block = es.enter_context(nc.Block())
        P.emit(block)
    return nc


def _regroup_win(w):
    w = np.asarray(w, np.float32)
    parts = [w[:, j * D:(j + 1) * D].reshape(D, NG, 128) for j in range(4)]
    out = np.stack(parts, axis=2)
    return np.ascontiguousarray(out.transpose(1, 0, 2, 3).reshape(NG, D, 512))


def kernel(x_prompt, x_sample, cache_k_0, cache_v_0, cache_k_1, cache_v_1,
           norm_g_0, w_in_0, q_norm_0, k_norm_0, lambda_q1_0, lambda_k1_0,
           lambda_q2_0, lambda_k2_0, subln_g_0, w_out_0, norm_g_1, w_in_1, w_out_1):
    f32 = np.float32
    x_prompt = np.asarray(x_prompt, f32)
    x_sample = np.asarray(x_sample, f32)
    c16, cf = _host_constants()
    win0 = _regroup_win(w_in_0)
    win1 = _regroup_win(w_in_1)
    wout0 = np.ascontiguousarray(np.asarray(w_out_0, f32))
    wout1 = np.ascontiguousarray(np.asarray(w_out_1, f32))
    gt0 = np.ascontiguousarray(np.broadcast_to(np.asarray(norm_g_0, f32)[None, :], (128, D)))
    gt1 = np.ascontiguousarray(np.broadcast_to(np.asarray(norm_g_1, f32)[None, :], (128, D)))
    cols = np.zeros((128, 4), f32)
    cols[:, 0] = np.tile(np.asarray(q_norm_0, f32), 2)
    cols[:, 1] = np.tile(np.asarray(k_norm_0, f32), 2)
    cols[:, 2] = np.asarray(subln_g_0, f32)
    lam = np.stack([np.asarray(v, f32) for v in (lambda_q1_0, lambda_k1_0, lambda_q2_0, lambda_k2_0)], axis=0)
    lamv = np.ascontiguousarray(np.broadcast_to(lam[None], (128, 4, 64)))
    caches = [np.asarray(c, f32).reshape(NCORES, PAST, D) for c in (cache_k_0, cache_v_0, cache_k_1, cache_v_1)]
    in_maps = []
    for c in range(NCORES):
        in_maps.append({
            "xp": np.ascontiguousarray(x_prompt[2 * c:2 * c + 2]),
            "xs": np.ascontiguousarray(x_sample[c]),
            "ck0": np.ascontiguousarray(caches[0][c]), "cv0": np.ascontiguousarray(caches[1][c]),
            "ck1": np.ascontiguousarray(caches[2][c]), "cv1": np.ascontiguousarray(caches[3][c]),
            "win0": win0, "win1": win1, "wout0": wout0, "wout1": wout1,
            "gt0": gt0, "gt1": gt1, "cols": cols, "lamv": lamv, "c16": c16, "cf32": cf,
        })
    nc = build_program()
    res = run_bass_kernel_spmd(nc, in_maps, core_ids=list(range(NCORES)))
    r = res.results

    def cat(name, shape):
        return np.concatenate([np.asarray(r[c][name], f32) for c in range(NCORES)], axis=0).reshape(shape)

    def stack(name, shape):
        return np.stack([np.asarray(r[c][name], f32) for c in range(NCORES)], axis=0).reshape(shape)

    y_prompt = cat("yp", (16, S, D))
    y_sample = stack("ys", (8, TS, D))
    k0p = cat("k0p", (16, S, 8, 128))
    v0p = cat("v0p", (16, S, 8, 128))
    k0s = stack("k0s", (8, TS, 8, 128))
    v0s = stack("v0s", (8, TS, 8, 128))
    k1p = cat("k1p", (16, S, 16, 64))
    v1p = cat("v1p", (16, S, 16, 64))
    k1s = stack("k1s", (8, TS, 16, 64))
    v1s = stack("v1s", (8, TS, 16, 64))
    return (y_prompt, y_sample, k0p, v0p, k0s, v0s, k1p, v1p, k1s, v1s)
```
